# Optimizing a Trainium2 kernel written in Bass

```python
import math
import jax
import jax.numpy as jnp
from jax import lax
import numpy as np

D_MODEL = 1024
BATCH = 8
SEQ = 4096
DEPTH = 2

S5_WIDTH = D_MODEL // 2
S5_GROUP_SIZE = 16
S5_GROUPS = S5_WIDTH // S5_GROUP_SIZE
S5_STATE = 64
S5_DT_MIN = 1e-3
S5_DT_MAX = 1e-1
POOL_WIDTH = D_MODEL // 2
POOL_WINDOWS = (2, 4, 8, 16)
POOL_GROUP_SIZE = POOL_WIDTH // len(POOL_WINDOWS)
ATTN_HEADS = 16
HEAD_DIM = 64
ATTN_WIDTH = ATTN_HEADS * HEAD_DIM
ATTN_PATTERNS = ((128, 1), (512, 4), (2048, 16))
ATTN_BLOCK = 64
N_BRANCHES = 3
IN_WIDTH = S5_WIDTH + POOL_WIDTH + 3 * ATTN_WIDTH + N_BRANCHES * D_MODEL
D_FF = 4 * D_MODEL
NORM_EPS = 1e-6
NEG_INF = -1e30

kernel_name = 'hybrid_s5_pool_dilated_attn_encoder'


def rmsnorm(x, g):
    xf = x.astype(jnp.float32)
    y = xf * lax.rsqrt(jnp.mean(xf * xf, axis=-1, keepdims=True) + NORM_EPS)
    return (y * g.astype(jnp.float32)).astype(x.dtype)


def _diag_recurrence(e1, e2):
    a1, b1 = e1
    a2, b2 = e2
    return a1 * a2, a2 * b1 + b2


def s5_direction(ug, lam_re, lam_im, log_dt, b_re, b_im, c_re, c_im, reverse):
    f32 = jnp.float32
    L = ug.shape[1]
    lam = lax.complex(lam_re.astype(f32), lam_im.astype(f32))
    dt = jnp.exp(log_dt.astype(f32))[:, None]
    a_bar = jnp.exp(lam * dt)
    b = lax.complex(b_re.astype(f32), b_im.astype(f32))
    b_bar = ((a_bar - 1.0) / lam)[:, :, None] * b
    bu = jnp.einsum('blgp,gnp->blgn', ug.astype(jnp.complex64), b_bar)
    if reverse:
        bu = jnp.flip(bu, axis=1)
    a_seq = jnp.broadcast_to(a_bar[None, None], (1, L) + a_bar.shape)
    _, states = lax.associative_scan(_diag_recurrence, (a_seq, bu), axis=1)
    if reverse:
        states = jnp.flip(states, axis=1)
    c = lax.complex(c_re.astype(f32), c_im.astype(f32))
    return jnp.einsum('blgn,gpn->blgp', states, c).real


def s5_mixer(u, lam_re, lam_im, log_dt, b_re, b_im, c_re, c_im, d_skip, w_glu):
    bsz, L, _ = u.shape
    uf = u.astype(jnp.float32)
    ug = uf.reshape(bsz, L, S5_GROUPS, S5_GROUP_SIZE)
    y_fwd = s5_direction(ug, lam_re[0], lam_im[0], log_dt[0], b_re[0], b_im[0], c_re[0], c_im[0], False)
    y_bwd = s5_direction(ug, lam_re[1], lam_im[1], log_dt[1], b_re[1], b_im[1], c_re[1], c_im[1], True)
    y = (y_fwd + y_bwd).reshape(bsz, L, S5_WIDTH) + d_skip.astype(jnp.float32) * uf
    y = jax.nn.gelu(y)
    y = y * jax.nn.sigmoid(y @ w_glu.astype(jnp.float32))
    return y.astype(u.dtype)


def pool_mixer(u, w_group, scale):
    bsz, L, _ = u.shape
    ng = len(POOL_WINDOWS)
    uf = u.astype(jnp.float32).reshape(bsz, L, ng, POOL_GROUP_SIZE)
    cs = jnp.concatenate([jnp.zeros((bsz, 1, ng, POOL_GROUP_SIZE), jnp.float32),
                          jnp.cumsum(uf, axis=1)], axis=1)
    t = jnp.arange(L)
    outs = []
    for g, win in enumerate(POOL_WINDOWS):
        left = win // 2
        right = win - 1 - left
        hi = jnp.minimum(t + right + 1, L)
        lo = jnp.maximum(t - left, 0)
        csg = cs[:, :, g]
        window_sum = jnp.take(csg, hi, axis=1) - jnp.take(csg, lo, axis=1)
        mean = window_sum / (hi - lo).astype(jnp.float32)[None, :, None]
        outs.append(mean - uf[:, :, g])
    pooled = jnp.stack(outs, axis=2)
    mixed = jnp.einsum('blgc,gcd->blgd', pooled, w_group.astype(jnp.float32))
    return (mixed.reshape(bsz, L, POOL_WIDTH) * scale.astype(jnp.float32)).astype(u.dtype)


def alibi_slopes(n_heads):
    return jnp.exp2(-8.0 * jnp.arange(1, n_heads + 1, dtype=jnp.float32) / n_heads)


def dilated_band(q, k, v, window, dil, slopes):
    bsz, L, H, E = q.shape
    radius = (window // 2) // dil
    QB = ATTN_BLOCK
    M = L // dil
    nb = -(-M // QB)
    Mp = nb * QB

    def by_residue(t):
        return t.reshape(bsz, M, dil, H, E).transpose(0, 2, 1, 3, 4)

    pad5 = lambda lo, hi: ((0, 0), (0, 0), (lo, hi), (0, 0), (0, 0))
    qb = jnp.pad(by_residue(q), pad5(0, Mp - M)).reshape(bsz, dil, nb, QB, H, E)
    kp = jnp.pad(by_residue(k), pad5(QB, Mp - M + QB)).reshape(bsz, dil, nb + 2, QB, H, E)
    vp = jnp.pad(by_residue(v), pad5(QB, Mp - M + QB)).reshape(bsz, dil, nb + 2, QB, H, E)
    kb = jnp.concatenate([kp[:, :, :-2], kp[:, :, 1:-1], kp[:, :, 2:]], axis=3)
    vb = jnp.concatenate([vp[:, :, :-2], vp[:, :, 1:-1], vp[:, :, 2:]], axis=3)

    a = jnp.arange(QB)
    c = jnp.arange(3 * QB)
    rel = c[None, :] - QB - a[:, None]
    key_m = jnp.arange(nb)[:, None] * QB + c[None, :] - QB
    valid = (jnp.abs(rel) <= radius)[None] & ((key_m >= 0) & (key_m < M))[:, None, :]
    dist = (dil * jnp.abs(rel)).astype(jnp.float32)

    scores = jnp.einsum('bdiqhe,bdikhe->bdihqk', qb, kb)
    scores = scores - slopes[:, None, None] * dist[None]
    scores = jnp.where(valid[None, None, :, None], scores, NEG_INF)
    mx = jnp.max(scores, axis=-1)
    p = jnp.exp(scores - mx[..., None])
    den = jnp.sum(p, axis=-1)
    num = jnp.einsum('bdihqk,bdikhe->bdiqhe', p, vb)

    num = num.reshape(bsz, dil, Mp, H, E)[:, :, :M].transpose(0, 2, 1, 3, 4).reshape(bsz, L, H, E)

    def back(s):
        s = s.transpose(0, 1, 2, 4, 3).reshape(bsz, dil, Mp, H)[:, :, :M]
        return s.transpose(0, 2, 1, 3).reshape(bsz, L, H)

    return num, back(mx), back(den)


def dilated_attention(qkv):
    bsz, L, _ = qkv.shape
    q, k, v = jnp.split(qkv.astype(jnp.float32), 3, axis=-1)
    q = q.reshape(bsz, L, ATTN_HEADS, HEAD_DIM) * (HEAD_DIM ** -0.5)
    k = k.reshape(bsz, L, ATTN_HEADS, HEAD_DIM)
    v = v.reshape(bsz, L, ATTN_HEADS, HEAD_DIM)
    slopes = alibi_slopes(ATTN_HEADS)
    nums, mxs, dens = [], [], []
    for window, dil in ATTN_PATTERNS:
        num, mx, den = dilated_band(q, k, v, window, dil, slopes)
        nums.append(num)
        mxs.append(mx)
        dens.append(den)
    mxs = jnp.stack(mxs)
    w = jnp.exp(mxs - jnp.max(mxs, axis=0, keepdims=True))
    total_num = jnp.sum(w[..., None] * jnp.stack(nums), axis=0)
    total_den = jnp.sum(w * jnp.stack(dens), axis=0)
    out = total_num / total_den[..., None]
    return out.reshape(bsz, L, ATTN_WIDTH).astype(qkv.dtype)


def setup_inputs(seed: int = 0) -> dict:
    key = jax.random.key(seed)
    ks = jax.random.split(key, 24)
    f32 = jnp.float32

    def nrm(k, shape, scale):
        return jax.random.normal(k, shape, f32) * scale

    G, N, P = S5_GROUPS, S5_STATE, S5_GROUP_SIZE
    n_idx = jnp.arange(N, dtype=f32)
    return {
        'x': nrm(ks[0], (BATCH, SEQ, D_MODEL), 1.0),
        'norm_mix': 1.0 + nrm(ks[1], (DEPTH, D_MODEL), 0.02),
        'w_in': nrm(ks[2], (DEPTH, D_MODEL, IN_WIDTH), D_MODEL ** -0.5),
        's5_lam_re': -0.5 + nrm(ks[3], (DEPTH, 2, G, N), 0.01),
        's5_lam_im': math.pi * n_idx + nrm(ks[4], (DEPTH, 2, G, N), 0.01),
        's5_log_dt': jax.random.uniform(ks[5], (DEPTH, 2, G), f32, math.log(S5_DT_MIN), math.log(S5_DT_MAX)),
        's5_b_re': nrm(ks[6], (DEPTH, 2, G, N, P), (2 * P) ** -0.5),
        's5_b_im': nrm(ks[7], (DEPTH, 2, G, N, P), (2 * P) ** -0.5),
        's5_c_re': nrm(ks[8], (DEPTH, 2, G, P, N), N ** -0.5),
        's5_c_im': nrm(ks[9], (DEPTH, 2, G, P, N), N ** -0.5),
        's5_d': nrm(ks[10], (DEPTH, S5_WIDTH), 1.0),
        's5_w_glu': nrm(ks[11], (DEPTH, S5_WIDTH, S5_WIDTH), S5_WIDTH ** -0.5),
        'pool_w': nrm(ks[12], (DEPTH, len(POOL_WINDOWS), POOL_GROUP_SIZE, POOL_GROUP_SIZE), POOL_GROUP_SIZE ** -0.5),
        'pool_scale': 1.0 + nrm(ks[13], (DEPTH, POOL_WIDTH), 0.1),
        'w_branch_s5': nrm(ks[14], (DEPTH, S5_WIDTH, D_MODEL), S5_WIDTH ** -0.5),
        'w_branch_pool': nrm(ks[15], (DEPTH, POOL_WIDTH, D_MODEL), POOL_WIDTH ** -0.5),
        'w_branch_attn': nrm(ks[16], (DEPTH, ATTN_WIDTH, D_MODEL), ATTN_WIDTH ** -0.5),
        'w_out': nrm(ks[17], (DEPTH, D_MODEL, D_MODEL), D_MODEL ** -0.5),
        'norm_mlp': 1.0 + nrm(ks[18], (DEPTH, D_MODEL), 0.02),
        'w_up': nrm(ks[19], (DEPTH, D_MODEL, D_FF), D_MODEL ** -0.5),
        'w_down': nrm(ks[20], (DEPTH, D_FF, D_MODEL), D_FF ** -0.5),
        'norm_final': 1.0 + nrm(ks[21], (D_MODEL,), 0.02),
    }


def reference(x, norm_mix, w_in, s5_lam_re, s5_lam_im, s5_log_dt, s5_b_re, s5_b_im,
              s5_c_re, s5_c_im, s5_d, s5_w_glu, pool_w, pool_scale, w_branch_s5,
              w_branch_pool, w_branch_attn, w_out, norm_mlp, w_up, w_down, norm_final):
    bsz, L, _ = x.shape
    split_at = [S5_WIDTH, S5_WIDTH + POOL_WIDTH, S5_WIDTH + POOL_WIDTH + 3 * ATTN_WIDTH]
    h = x
    for l in range(DEPTH):
        xn = rmsnorm(h, norm_mix[l])
        proj = xn @ w_in[l]
        u_s5, u_pool, qkv, gate_logits = jnp.split(proj, split_at, axis=-1)
        y_s5 = s5_mixer(u_s5, s5_lam_re[l], s5_lam_im[l], s5_log_dt[l], s5_b_re[l], s5_b_im[l],
                        s5_c_re[l], s5_c_im[l], s5_d[l], s5_w_glu[l])
        y_pool = pool_mixer(u_pool, pool_w[l], pool_scale[l])
        y_attn = dilated_attention(qkv)
        gates = jax.nn.sigmoid(gate_logits).reshape(bsz, L, N_BRANCHES, D_MODEL)
        merged = (gates[:, :, 0] * (y_s5 @ w_branch_s5[l])
                  + gates[:, :, 1] * (y_pool @ w_branch_pool[l])
                  + gates[:, :, 2] * (y_attn @ w_branch_attn[l]))
        h = h + merged @ w_out[l]
        hn = rmsnorm(h, norm_mlp[l])
        h = h + jnp.square(jax.nn.relu(hn @ w_up[l])) @ w_down[l]
    return rmsnorm(h, norm_final)
```

```python
import math
import numpy as np
import concourse.bass as bass
import concourse.mybir as mybir
from concourse.bass_utils import run_bass_kernel_spmd
from contextlib import ExitStack

F32 = mybir.dt.float32
BF16 = mybir.dt.bfloat16
AF = mybir.ActivationFunctionType
ALU = mybir.AluOpType

L = 4096
DM = 1024
DEPTH = 2
NCORES = 8
EPS = 1e-6
TWO_PI = 2.0 * math.pi
PATTERNS = (1, 4, 16)
KPAD = 1024
POOL_WINS = (2, 4, 8, 16)


class Buf:
    __slots__ = ("w", "r", "name", "excl")

    def __init__(self, name="", excl=False):
        self.w = None
        self.r = {}
        self.name = name
        self.excl = excl


class _Rec:
    def __init__(self):
        self.call = None

    def __getattr__(self, name):
        def f(*a, **k):
            self.call = (name, a, k)
            return None
        return f


class Sched:
    COMPUTE = ("pe", "act", "dve", "pool")

    def __init__(self, nc, n_dma_slots=14):
        self.nc = nc
        self.ops = []
        self.n_dma_slots = n_dma_slots

    def op(self, stream, fn, reads=(), writes=(), dma=False):
        rd = tuple(b for b in reads if not b.excl)
        wr = tuple(writes) + tuple(b for b in reads if b.excl)
        rec = _Rec()
        fn(rec)
        self.ops.append((stream, rec.call, rd, wr, dma))

    def barrier(self):
        self.ops.append(("barrier", None, (), (), False))

    def emit(self, es):
        nc = self.nc
        eng = {"pe": nc.tensor, "act": nc.scalar, "dve": nc.vector, "pool": nc.gpsimd, "sp": nc.sync}
        ops = self.ops
        n = len(ops)
        deps = [None] * n
        signal = [False] * n
        pos_in_stream = [0] * n
        cnt = {}
        last_compute = {}
        for i, (st, fn, reads, writes, dma) in enumerate(ops):
            if st == "barrier":
                d = set(last_compute.values())
                deps[i] = d
                for j in d:
                    signal[j] = True
                continue
            cnt[st] = cnt.get(st, 0) + 1
            pos_in_stream[i] = cnt[st]
            d = set()
            for b in reads:
                if b.w is not None:
                    d.add(b.w)
            for b in writes:
                if b.w is not None:
                    d.add(b.w)
                for j in b.r.values():
                    d.add(j)
            dd = set()
            for j in d:
                sj = ops[j][0]
                jd = ops[j][4]
                if sj == st and not jd and not dma:
                    if st == "pe":
                        continue
                dd.add(j)
            deps[i] = dd
            for j in dd:
                signal[j] = True
            for b in reads:
                b.r[(st, i) if dma else st] = i
            for b in writes:
                b.w = i
                b.r = {}
            if not dma:
                last_compute[st] = i
        sems = {}
        for st in self.COMPUTE:
            sems[st] = es.enter_context(nc.semaphore("s_" + st))
        dma_sems = [es.enter_context(nc.semaphore("s_dma%d" % k)) for k in range(self.n_dma_slots)]
        dma_slot_val = [0] * self.n_dma_slots
        dma_next = 0
        dma_next_sw = 0
        n_hw = 9
        sig_val = [None] * n
        counts = {st: 0 for st in sems}
        waited = {}

        def do_wait(st, sem, key, val):
            k = (st, key)
            if waited.get(k, 0) >= val:
                return
            waited[k] = val
            eng[st].wait_ge(sem, val)

        streams_all = ("pe", "act", "dve", "pool", "sp")
        for i, (st, fn, reads, writes, dma) in enumerate(ops):
            if st == "barrier":
                for s2 in streams_all:
                    for j in deps[i]:
                        sem, val, key = sig_val[j]
                        do_wait(s2, sem, key, val)
                    for slot in range(self.n_dma_slots):
                        if dma_slot_val[slot] > 0:
                            do_wait(s2, dma_sems[slot], ("dma", slot), dma_slot_val[slot])
                continue
            for j in sorted(deps[i]):
                sem, val, key = sig_val[j]
                do_wait(st, sem, key, val)
            if dma:
                if st == "sp":
                    slot = dma_next
                    dma_next = (dma_next + 1) % n_hw
                else:
                    slot = n_hw + dma_next_sw
                    dma_next_sw = (dma_next_sw + 1) % (self.n_dma_slots - n_hw)
                if dma_slot_val[slot] > 0:
                    do_wait(st, dma_sems[slot], ("dma", slot), dma_slot_val[slot])
                ins = getattr(eng[st], fn[0])(*fn[1], **fn[2])
                dma_slot_val[slot] += 16
                ins.then_inc(dma_sems[slot], 16)
                sig_val[i] = (dma_sems[slot], dma_slot_val[slot], ("dma", slot))
            else:
                ins = getattr(eng[st], fn[0])(*fn[1], **fn[2])
                if signal[i]:
                    counts[st] += 1
                    ins.then_inc(sems[st], 1)
                    sig_val[i] = (sems[st], counts[st], st)
        for slot in range(self.n_dma_slots):
            if dma_slot_val[slot] > 0:
                do_wait("sp", dma_sems[slot], ("dma", slot), dma_slot_val[slot])
        return counts


class Rot:
    def __init__(self, items):
        self.items = items
        self.i = 0

    def next(self):
        it = self.items[self.i]
        self.i = (self.i + 1) % len(self.items)
        return it


def build_program(debug=None, stop_after=None):
    nc = bass.Bass("TRN2", target_bir_lowering=False)
    S = Sched(nc)
    dbg_out = {}

    def din(name, shape, dt=F32):
        return nc.dram_tensor(name, list(shape), dt, kind="ExternalInput").ap()

    def dscr(name, shape, dt):
        return nc.dram_tensor(name, list(shape), dt, kind="Internal").ap()

    xT = din("xT", [DM, L])
    w_in = din("w_in", [DEPTH, DM, 7168])
    g_mix = din("g_mix", [128, DEPTH, 8])
    g_mlp = din("g_mlp", [128, DEPTH, 8])
    g_fin = din("g_fin", [128, 8])
    lam_re = din("lam_re", [128, DEPTH, 32])
    lam_im = din("lam_im", [128, DEPTH, 32])
    log_dt = din("log_dt", [128, DEPTH, 32])
    b_re = din("b_re", [128, DEPTH, 32, 16])
    b_im = din("b_im", [128, DEPTH, 32, 16])
    c_re = din("c_re", [128, DEPTH, 32, 16])
    c_im = din("c_im", [128, DEPTH, 32, 16])
    s5_d = din("s5_d", [128, DEPTH, 4])
    w_glu = din("w_glu", [DEPTH, 512, 512])
    pool_w = din("pool_w", [DEPTH, 4, 128, 128])
    pool_scale = din("pool_scale", [128, DEPTH, 4])
    w_bs5 = din("w_bs5", [DEPTH, 512, DM])
    w_bpool = din("w_bpool", [DEPTH, 512, DM])
    w_battn = din("w_battn", [DEPTH, DM, DM])
    w_out = din("w_out", [DEPTH, DM, DM])
    w_up = din("w_up", [DEPTH, DM, 4096])
    w_down = din("w_down", [DEPTH, 4096, DM])
    ident_d = din("ident", [128, 128])
    maskg_d = din("maskg", [128, 2])
    wmask_d = din("wmask", [128, 48, 256])
    outT = nc.dram_tensor("outT", [DM, L], F32, kind="ExternalOutput").ap()
    if debug:
        for name, shape in debug.items():
            dbg_out[name] = nc.dram_tensor("dbg_" + name, list(shape), F32, kind="ExternalOutput").ap()

    hT = dscr("hT", [DM, L], F32)
    xn_d = dscr("xn_d", [DM, L], BF16)
    q_d = dscr("q_d", [DM, L], BF16)
    k_d = dscr("k_d", [DM, L], BF16)
    v_d = dscr("v_d", [DM, L], BF16)
    ys5_d = dscr("ys5_d", [512, L], BF16)
    upool_d = dscr("upool_d", [512, L], BF16)
    B_upd = Buf("upool_d")
    y1_d = dscr("y1_d", [512, L], BF16)
    ypool_d = dscr("ypool_d", [512, L], BF16)
    yattn_d = dscr("yattn_d", [DM, L], BF16)
    B_hT = [Buf("hT%d" % i) for i in range(16)]
    B_xn = [Buf("xn%d" % i) for i in range(16)]
    B_q, B_k, B_v = Buf("q"), Buf("k"), Buf("v")
    B_ys5 = [Buf() for _ in range(8)]
    B_ypool = [Buf() for _ in range(8)]
    B_yattn = Buf("yattn")
    B_out = Buf("out")
    B_dbg = Buf("dbg")

    def r_kpt(ap):
        return ap.rearrange("(k p) t -> p k t", p=128)

    with ExitStack() as es:
        uid = [0]

        def sbt(stack, name, shape, dt):
            uid[0] += 1
            return stack.enter_context(nc.sbuf_tensor("sb%d_%s" % (uid[0], name), list(shape), dt))

        ones_bf = sbt(es, "ones_bf", [128, 128], BF16)
        ones_f = sbt(es, "ones_f", [128, 64], F32)
        ident_f = sbt(es, "ident_f", [128, 128], F32)
        ident_b = sbt(es, "ident_b", [128, 128], BF16)
        maskg = sbt(es, "maskg", [128, 2], F32)
        gmix = sbt(es, "gmix", [128, DEPTH, 8], F32)
        gmlp = sbt(es, "gmlp", [128, DEPTH, 8], F32)
        gfin = sbt(es, "gfin", [128, 8], F32)
        B_const = Buf("const")
        psum = []
        for i in range(6):
            t = es.enter_context(nc.psum_tensor("ps%d" % i, [128, 512], F32))
            psum.append((t, Buf("ps%d" % i, excl=True)))
        psbfs = [(es.enter_context(nc.psum_tensor("psbf%d" % i, [128, 1024], BF16)), Buf("psbf%d" % i, excl=True)) for i in range(2)]
        PSB = Rot(psbfs)
        PS = Rot(psum)

        eps_t = sbt(es, "eps_t", [128, 1], F32)
        S.op("dve", lambda e: e.memset(eps_t[:], EPS), writes=[B_const])
        S.op("dve", lambda e: e.memset(ones_bf[:], 1.0), writes=[B_const])
        S.op("dve", lambda e: e.memset(ones_f[:], 1.0), writes=[B_const])
        S.op("sp", lambda e: e.dma_start(out=ident_f[:], in_=ident_d), writes=[B_const], dma=True)
        S.op("pool", lambda e: e.dma_start(out=ident_b[:], in_=ident_d), writes=[B_const], dma=True)
        S.op("sp", lambda e: e.dma_start(out=maskg[:], in_=maskg_d), writes=[B_const], dma=True)
        S.op("sp", lambda e: e.dma_start(out=gmix[:], in_=g_mix), writes=[B_const], dma=True)
        S.op("sp", lambda e: e.dma_start(out=gmlp[:], in_=g_mlp), writes=[B_const], dma=True)
        S.op("sp", lambda e: e.dma_start(out=gfin[:], in_=g_fin), writes=[B_const], dma=True)
        S.barrier()

        def dbg_store(name, src_ap, src_buf, eng="sp"):
            if debug and name in dbg_out:
                S.op("sp", lambda e: e.dma_start(out=dbg_out[name], in_=src_ap), reads=[src_buf], writes=[B_dbg], dma=True)

        def dbg_dump(name, src, bufs):
            if debug and name in dbg_out:
                S.barrier()
                if src.dtype == F32:
                    S.op("sp", lambda e: e.dma_start(out=dbg_out[name], in_=src), reads=list(bufs), writes=[B_dbg], dma=True)
                else:
                    S.op("pool", lambda e: e.dma_start(out=dbg_out[name].rearrange("r (a b) -> r a b", b=2048), in_=src.rearrange("r (a b) -> r a b", b=2048)),
                         reads=list(bufs), writes=[B_dbg], dma=True)
                S.barrier()

        def emit_norm(h_t, hb, g_ap, out_t, ob, sq_t, sqb, rs_t, rsb, nt):
            S.op("act", lambda e: e.activation(out=sq_t[:, :, :nt], in_=h_t[:, :, :nt], func=AF.Square), reads=[hb], writes=[sqb])
            ps, pb = PS.next()
            for k in range(8):
                S.op("pe", lambda e, k=k: e.matmul(ps[:, :nt], lhsT=ones_bf[:, :], rhs=sq_t[:, k, :nt], start=(k == 0), stop=(k == 7)),
                     reads=[sqb], writes=[pb])
            S.op("act", lambda e: e.activation(out=rs_t[:, :nt], in_=ps[:, :nt], func=AF.Ln, scale=1.0 / DM, bias=eps_t[:, 0:1]), reads=[pb], writes=[rsb])
            S.op("act", lambda e: e.activation(out=rs_t[:, :nt], in_=rs_t[:, :nt], func=AF.Exp, scale=-0.5), reads=[rsb], writes=[rsb])
            for k in range(8):
                S.op("dve", lambda e, k=k: e.scalar_tensor_tensor(out=out_t[:, k, :nt], in0=h_t[:, k, :nt], scalar=g_ap[:, k:k + 1], in1=rs_t[:, :nt],
                                                                   op0=ALU.mult, op1=ALU.mult), reads=[hb, rsb, B_const], writes=[ob])


        for l in range(DEPTH):
            with ExitStack() as ph_mix:
                u5r = sbt(ph_mix, "u5r", [128, 4, 8, 512], BF16)
                B_u5 = [[Buf() for _ in range(8)] for _ in range(4)]
                with ExitStack() as ph:
                    with ExitStack() as pa:
                        wA = sbt(pa, "a_w", [128, 8, 4096], BF16)
                        B_wA = [Buf() for _ in range(8)]
                        xts = [(sbt(pa, "a_xn%d" % i, [128, 8, 512], BF16), Buf()) for i in range(2)]
                        if l == 0:
                            hts0 = [(sbt(pa, "a_h%d" % i, [128, 8, 512], F32), Buf()) for i in range(2)]
                            sq0 = sbt(pa, "a_sq", [128, 8, 512], BF16); sq0b = Buf()
                            rs0 = sbt(pa, "a_rs", [128, 512], F32); rs0b = Buf()

                        def prep_load(tt):
                            sl_ = slice(tt * 512, (tt + 1) * 512)
                            if l == 0:
                                h_t, hb = hts0[tt % 2]
                                S.op("sp", lambda e: e.dma_start(out=h_t[:], in_=r_kpt(xT)[:, :, sl_]), writes=[hb], dma=True)
                                S.op("sp", lambda e: e.dma_start(out=r_kpt(hT)[:, :, sl_], in_=h_t[:]), reads=[hb], writes=[B_hT[2 * tt], B_hT[2 * tt + 1]], dma=True)
                            else:
                                x_t, xb_ = xts[tt % 2]
                                S.op("sp", lambda e: e.dma_start(out=x_t[:], in_=r_kpt(xn_d)[:, :, sl_]), reads=[B_xn[2 * tt], B_xn[2 * tt + 1]], writes=[xb_], dma=True)

                        def prep_norm(tt, stage):
                            if l != 0:
                                return
                            sl_ = slice(tt * 512, (tt + 1) * 512)
                            h_t, hb = hts0[tt % 2]
                            x_t, xb_ = xts[tt % 2]
                            if stage == 0:
                                S.op("act", lambda e: e.activation(out=sq0[:], in_=h_t[:], func=AF.Square), reads=[hb], writes=[sq0b])
                                return
                            ps, pb = PS.next()
                            for k in range(8):
                                S.op("pe", lambda e, k=k: e.matmul(ps[:, :], lhsT=ones_bf[:, :], rhs=sq0[:, k, :], start=(k == 0), stop=(k == 7)), reads=[sq0b], writes=[pb])
                            S.op("act", lambda e: e.activation(out=rs0[:, :], in_=ps[:, :], func=AF.Ln, scale=1.0 / DM, bias=eps_t[:, 0:1]), reads=[pb, B_const], writes=[rs0b])
                            S.op("act", lambda e: e.activation(out=rs0[:, :], in_=rs0[:, :], func=AF.Exp, scale=-0.5), reads=[rs0b], writes=[rs0b])
                            for k in range(8):
                                S.op("dve", lambda e, k=k: e.scalar_tensor_tensor(out=x_t[:, k, :], in0=h_t[:, k, :], scalar=gmix[:, 0, k:k + 1], in1=rs0[:, :], op0=ALU.mult, op1=ALU.mult),
                                     reads=[hb, rs0b, B_const], writes=[xb_])
                            S.op("sp", lambda e: e.dma_start(out=r_kpt(xn_d)[:, :, sl_], in_=x_t[:]), reads=[xb_], writes=[B_xn[2 * tt], B_xn[2 * tt + 1]], dma=True)

                        prep_load(0)
                        prep_norm(0, 0)
                        prep_norm(0, 1)
                        for cb in range(8):
                            S.op("pool", lambda e, cb=cb: e.dma_start(out=wA[:, :, cb * 512:(cb + 1) * 512], in_=w_in[l].rearrange("(k p) c -> p k c", p=128)[:, :, cb * 512:(cb + 1) * 512]),
                                 writes=[B_wA[cb]], dma=True)
                        sts = [(sbt(pa, "a_st%d" % i, [128, 4, 512], BF16), Buf()) for i in range(3)]
                        STG = Rot(sts)
                        ev = 0
                        for tt in range(8):
                            sl = slice(tt * 512, (tt + 1) * 512)
                            x_t, xb_ = xts[tt % 2]
                            if tt + 1 < 8:
                                prep_load(tt + 1)
                            for cb in range(8):
                                if tt + 1 < 8 and cb == 2:
                                    prep_norm(tt + 1, 0)
                                if tt + 1 < 8 and cb == 4:
                                    prep_norm(tt + 1, 1)
                                if cb >= 1:
                                    st_t, stb = STG.next()
                                for m in range(4):
                                    ps, pb = PS.next()
                                    for k in range(8):
                                        S.op("pe", lambda e, ps=ps, cb=cb, m=m, k=k, x_t=x_t: e.matmul(ps[:, :], lhsT=wA[:, k, cb * 512 + m * 128:cb * 512 + (m + 1) * 128], rhs=x_t[:, k, :],
                                                                                                    start=(k == 0), stop=(k == 7)),
                                             reads=[B_wA[cb], xb_], writes=[pb])
                                    if cb == 0:
                                        dst, db = u5r[:, m, :, tt * 64:(tt + 1) * 64], B_u5[m][tt]
                                        src = ps[:, :].rearrange("p (c t) -> p t c", t=8)
                                    else:
                                        dst, db = st_t[:, m, :], stb
                                        src = ps[:, :]
                                    sc = 0.125 if cb in (2, 3) else 1.0
                                    if ev % 2 == 0:
                                        S.op("act", lambda e, dst=dst, src=src, sc=sc: e.activation(out=dst, in_=src, func=AF.Copy, scale=sc), reads=[pb], writes=[db])
                                    else:
                                        S.op("dve", lambda e, dst=dst, src=src, sc=sc: e.tensor_scalar(out=dst, in0=src, scalar1=sc, scalar2=None, op0=ALU.mult), reads=[pb], writes=[db])
                                    ev += 1
                                if cb == 1:
                                    S.op("sp", lambda e, sl=sl, st_t=st_t: e.dma_start(out=upool_d[:, sl].rearrange("(m p) t -> p m t", p=128), in_=st_t[:]),
                                         reads=[stb], writes=[B_upd], dma=True)
                                if cb >= 2:
                                    dd, dbuf = ((q_d, B_q), (k_d, B_k), (v_d, B_v))[(cb - 2) // 2]
                                    r0 = ((cb - 2) % 2) * 512
                                    S.op("sp", lambda e, dd=dd, r0=r0, sl=sl, st_t=st_t: e.dma_start(out=dd[r0:r0 + 512, sl].rearrange("(m p) t -> p m t", p=128), in_=st_t[:]),
                                         reads=[stb], writes=[dbuf], dma=True)
                    S.barrier()
                    if debug and "u5" in dbg_out:
                        with ExitStack() as pd:
                            tmp = sbt(pd, "dbg_u5", [128, 4, L], F32); tb_ = Buf()
                            S.op("dve", lambda e: e.tensor_copy(out=tmp[:].rearrange("p j (c t) -> p j t c", t=8), in_=u5r[:]), writes=[tb_])
                            S.op("sp", lambda e: e.dma_start(out=dbg_out["u5"].rearrange("(j p) t -> p j t", p=128), in_=tmp[:]), reads=[tb_], writes=[B_dbg], dma=True)
                        S.barrier()


                with ExitStack() as p5:
                    FS = sbt(p5, "FS", [128, 2, 2, 16, 512], BF16)
                    B_FS = [Buf("FSf"), Buf("FSb")]
                    B_Y1 = [[Buf() for _ in range(8)] for _ in range(4)]
                    prm = sbt(p5, "prm", [128, 16, 32], F32)
                    Bp = Buf("prm")
                    Pre = sbt(p5, "Pre", [128, 9, 32], F32)
                    Pim = sbt(p5, "Pim", [128, 9, 32], F32)
                    BB = sbt(p5, "BB", [128, 2, 32, 16], F32)
                    CC = sbt(p5, "CC", [128, 2, 32, 16], F32)
                    dsk = sbt(p5, "dsk", [128, 4], F32)
                    tmpA = sbt(p5, "tmpA", [128, 32, 16], F32); tAb = Buf()
                    tmpB = sbt(p5, "tmpB", [128, 32, 16], F32); tBb = Buf()
                    LRE, LIM, DT, ER, TH, T1, T2, SN, CS, NR, NI, DEN, C0R, C0I, AM1, NAI = range(16)

                    def P_(i):
                        return prm[:, i, :]

                    def vop(fn, reads=(Bp,), writes=(Bp,), eng="dve"):
                        S.op(eng, fn, reads=list(reads), writes=list(writes))

                    S.op("sp", lambda e: e.dma_start(out=prm[:, LRE, :], in_=lam_re[:, l, :]), writes=[Bp], dma=True)
                    S.op("sp", lambda e: e.dma_start(out=prm[:, LIM, :], in_=lam_im[:, l, :]), writes=[Bp], dma=True)
                    S.op("sp", lambda e: e.dma_start(out=prm[:, DT, :], in_=log_dt[:, l, :]), writes=[Bp], dma=True)
                    S.op("sp", lambda e: e.dma_start(out=BB[:, 0], in_=b_re[:, l]), writes=[Bp], dma=True)
                    S.op("sp", lambda e: e.dma_start(out=BB[:, 1], in_=b_im[:, l]), writes=[Bp], dma=True)
                    S.op("sp", lambda e: e.dma_start(out=CC[:, 0], in_=c_re[:, l]), writes=[Bp], dma=True)
                    S.op("sp", lambda e: e.dma_start(out=CC[:, 1], in_=c_im[:, l]), writes=[Bp], dma=True)
                    S.op("sp", lambda e: e.dma_start(out=dsk[:], in_=s5_d[:, l, :]), writes=[Bp], dma=True)
                    S.barrier()
                    S.op("act", lambda e: e.activation(out=P_(DT), in_=P_(DT), func=AF.Exp), reads=[Bp], writes=[Bp])
                    vop(lambda e: e.tensor_tensor(out=P_(ER), in0=P_(LRE), in1=P_(DT), op=ALU.mult))
                    S.op("act", lambda e: e.activation(out=P_(ER), in_=P_(ER), func=AF.Exp), reads=[Bp], writes=[Bp])
                    vop(lambda e: e.tensor_tensor(out=P_(TH), in0=P_(LIM), in1=P_(DT), op=ALU.mult))
                    prm_i = sbt(p5, "prm_i", [128, 32], mybir.dt.int32)
                    for (off, dst) in ((math.pi, SN), (1.5 * math.pi, CS)):
                        vop(lambda e, off=off: e.tensor_scalar(out=P_(T1), in0=P_(TH), scalar1=off, scalar2=1.0 / TWO_PI, op0=ALU.add, op1=ALU.mult))
                        vop(lambda e: e.tensor_copy(out=prm_i[:, :], in_=P_(T1)))
                        vop(lambda e: e.tensor_copy(out=P_(T2), in_=prm_i[:, :]))
                        vop(lambda e: e.tensor_tensor(out=P_(T2), in0=P_(T1), in1=P_(T2), op=ALU.subtract))
                        vop(lambda e: e.tensor_scalar(out=P_(T1), in0=P_(T2), scalar1=0.0, scalar2=None, op0=ALU.is_lt))
                        vop(lambda e: e.tensor_tensor(out=P_(T2), in0=P_(T2), in1=P_(T1), op=ALU.add))
                        vop(lambda e: e.tensor_scalar(out=P_(T2), in0=P_(T2), scalar1=TWO_PI, scalar2=-math.pi, op0=ALU.mult, op1=ALU.add))
                        vop(lambda e: e.tensor_scalar(out=P_(T2), in0=P_(T2), scalar1=math.pi, scalar2=-math.pi, op0=ALU.min, op1=ALU.max))
                        S.op("act", lambda e, dst=dst: e.activation(out=P_(dst), in_=P_(T2), func=AF.Sin), reads=[Bp], writes=[Bp])
                    vop(lambda e: e.memset(Pre[:, 0, :], 1.0))
                    vop(lambda e: e.memset(Pim[:, 0, :], 0.0))
                    vop(lambda e: e.tensor_tensor(out=Pre[:, 1, :], in0=P_(ER), in1=P_(CS), op=ALU.mult))
                    vop(lambda e: e.tensor_tensor(out=Pim[:, 1, :], in0=P_(ER), in1=P_(SN), op=ALU.mult))
                    for k in range(2, 9):
                        vop(lambda e, k=k: e.tensor_tensor(out=P_(T1), in0=Pre[:, k - 1, :], in1=Pre[:, 1, :], op=ALU.mult))
                        vop(lambda e, k=k: e.tensor_tensor(out=P_(T2), in0=Pim[:, k - 1, :], in1=Pim[:, 1, :], op=ALU.mult))
                        vop(lambda e, k=k: e.tensor_tensor(out=Pre[:, k, :], in0=P_(T1), in1=P_(T2), op=ALU.subtract))
                        vop(lambda e, k=k: e.tensor_tensor(out=P_(T1), in0=Pre[:, k - 1, :], in1=Pim[:, 1, :], op=ALU.mult))
                        vop(lambda e, k=k: e.tensor_tensor(out=P_(T2), in0=Pim[:, k - 1, :], in1=Pre[:, 1, :], op=ALU.mult))
                        vop(lambda e, k=k: e.tensor_tensor(out=Pim[:, k, :], in0=P_(T1), in1=P_(T2), op=ALU.add))
                    vop(lambda e: e.tensor_scalar(out=P_(AM1), in0=Pre[:, 1, :], scalar1=-1.0, scalar2=None, op0=ALU.add))
                    vop(lambda e: e.tensor_tensor(out=P_(T1), in0=P_(AM1), in1=P_(LRE), op=ALU.mult))
                    vop(lambda e: e.tensor_tensor(out=P_(T2), in0=Pim[:, 1, :], in1=P_(LIM), op=ALU.mult))
                    vop(lambda e: e.tensor_tensor(out=P_(NR), in0=P_(T1), in1=P_(T2), op=ALU.add))
                    vop(lambda e: e.tensor_tensor(out=P_(T1), in0=Pim[:, 1, :], in1=P_(LRE), op=ALU.mult))
                    vop(lambda e: e.tensor_tensor(out=P_(T2), in0=P_(AM1), in1=P_(LIM), op=ALU.mult))
                    vop(lambda e: e.tensor_tensor(out=P_(NI), in0=P_(T1), in1=P_(T2), op=ALU.subtract))
                    vop(lambda e: e.tensor_tensor(out=P_(T1), in0=P_(LRE), in1=P_(LRE), op=ALU.mult))
                    vop(lambda e: e.tensor_tensor(out=P_(T2), in0=P_(LIM), in1=P_(LIM), op=ALU.mult))
                    vop(lambda e: e.tensor_tensor(out=P_(DEN), in0=P_(T1), in1=P_(T2), op=ALU.add))
                    vop(lambda e: e.reciprocal(out=P_(DEN), in_=P_(DEN)))
                    vop(lambda e: e.tensor_tensor(out=P_(C0R), in0=P_(NR), in1=P_(DEN), op=ALU.mult))
                    vop(lambda e: e.tensor_tensor(out=P_(C0I), in0=P_(NI), in1=P_(DEN), op=ALU.mult))

                    def bc(ap32):
                        n_ = ap32.shape[1]
                        return ap32.unsqueeze(2).to_broadcast([128, n_, 16])

                    def cmul(dst_re, dst_im, x_re, x_im, s_re, s_im, n_):
                        vop(lambda e: e.tensor_tensor(out=tmpA[:, :n_, :], in0=x_re, in1=bc(s_re), op=ALU.mult), reads=[Bp], writes=[tAb])
                        vop(lambda e: e.tensor_tensor(out=tmpB[:, :n_, :], in0=x_im, in1=bc(s_im), op=ALU.mult), reads=[Bp], writes=[tBb])
                        vop(lambda e: e.tensor_tensor(out=dst_re, in0=tmpA[:, :n_, :], in1=tmpB[:, :n_, :], op=ALU.subtract), reads=[tAb, tBb], writes=[Bp])
                        vop(lambda e: e.tensor_tensor(out=tmpA[:, :n_, :], in0=x_re, in1=bc(s_im), op=ALU.mult), reads=[Bp], writes=[tAb])
                        vop(lambda e: e.tensor_tensor(out=tmpB[:, :n_, :], in0=x_im, in1=bc(s_re), op=ALU.mult), reads=[Bp], writes=[tBb])
                        vop(lambda e: e.tensor_tensor(out=dst_im, in0=tmpA[:, :n_, :], in1=tmpB[:, :n_, :], op=ALU.add), reads=[tAb, tBb], writes=[Bp])

                    with ExitStack() as pbt_:
                        BBt = sbt(pbt_, "BBt", [128, 2, 32, 16], F32)
                        cmul(BBt[:, 0], BBt[:, 1], BB[:, 0], BB[:, 1], P_(C0R), P_(C0I), 32)
                        vop(lambda e: e.tensor_copy(out=BB[:], in_=BBt[:]))
                    S.barrier()

                    with ExitStack() as pw_:
                        WD = [(sbt(pw_, "Wd%d" % i, [128, 2, 2, 8, 128], BF16), sbt(pw_, "W3%d" % i, [128, 2, 2, 8, 128], BF16), Buf("Wd%d" % i)) for i in range(2)]
                        Md = sbt(pw_, "Md", [128, 2, 8, 8, 16], F32)
                        B_Md = Buf("Md")
                        Xd = sbt(pw_, "Xd", [128, 2, 8, 8, 2, 16], F32)
                        B_Xd = Buf("Xd")
                        tA4 = tmpA[:].rearrange("p (a b) c -> p a b c", a=8)
                        tB4 = tmpB[:].rearrange("p (a b) c -> p a b c", a=8)

                        def build_w(j):
                            Wd, W3, B_Wd = WD[j % 2]
                            for d in range(2):
                                cs = slice(d * 16 + 4 * j, d * 16 + 4 * j + 4)
                                ds = slice(d * 4, d * 4 + 4)
                                esl = slice(7, None, -1) if d == 0 else slice(0, 8)
                                sr = Pre[:, esl, cs].unsqueeze(3).to_broadcast([128, 8, 4, 16])
                                si = Pim[:, esl, cs].unsqueeze(3).to_broadcast([128, 8, 4, 16])
                                xr = BB[:, 0, cs].unsqueeze(1).to_broadcast([128, 8, 4, 16])
                                xi = BB[:, 1, cs].unsqueeze(1).to_broadcast([128, 8, 4, 16])
                                S.op("dve", lambda e, xr=xr, sr=sr: e.tensor_tensor(out=tA4, in0=xr, in1=sr, op=ALU.mult), reads=[Bp], writes=[tAb])
                                S.op("dve", lambda e, xi=xi, si=si: e.tensor_tensor(out=tB4, in0=xi, in1=si, op=ALU.mult), reads=[Bp], writes=[tBb])
                                S.op("dve", lambda e, ds=ds: e.tensor_tensor(out=Md[:, 0, :, ds, :], in0=tA4, in1=tB4, op=ALU.subtract), reads=[tAb, tBb], writes=[B_Md])
                                S.op("dve", lambda e, xr=xr, si=si: e.tensor_tensor(out=tA4, in0=xr, in1=si, op=ALU.mult), reads=[Bp], writes=[tAb])
                                S.op("dve", lambda e, xi=xi, sr=sr: e.tensor_tensor(out=tB4, in0=xi, in1=sr, op=ALU.mult), reads=[Bp], writes=[tBb])
                                S.op("dve", lambda e, ds=ds: e.tensor_tensor(out=Md[:, 1, :, ds, :], in0=tA4, in1=tB4, op=ALU.add), reads=[tAb, tBb], writes=[B_Md])
                            for part in range(2):
                                for gl in range(2):
                                    S.op("dve", lambda e, part=part, gl=gl: e.tensor_scalar(out=Xd[:, part, :, :, gl, :], in0=Md[:, part], scalar1=maskg[:, gl:gl + 1], scalar2=None, op0=ALU.mult),
                                         reads=[B_Md, B_const], writes=[B_Xd])
                            for dl in range(8):
                                for part in range(2):
                                    for d in range(2):
                                        ps, pb = PS.next()
                                        S.op("pe", lambda e, ps=ps, part=part, d=d, dl=dl: e.transpose(out=ps[:, 0:128], in_=Xd[:, part, dl, d * 4:d * 4 + 4].rearrange("p a b c -> p (a b c)"), identity=ident_f[:, :]),
                                             reads=[B_Xd, B_const], writes=[pb])
                                        S.op("act", lambda e, ps=ps, d=d, part=part, dl=dl, Wd=Wd: e.activation(out=Wd[:, d, part, dl, :], in_=ps[:, 0:128], func=AF.Copy), reads=[pb], writes=[B_Wd])
                                        S.op("dve", lambda e, ps=ps, d=d, part=part, dl=dl, W3=W3: e.tensor_copy(out=W3[64:128, d, part, dl, :], in_=ps[64:128, 0:128]), reads=[pb], writes=[B_Wd])
                            S.op("pool", lambda e, W3=W3: e.memset(W3[64:96].rearrange("p a b c d -> p (a b c d)"), 0.0), writes=[B_Wd])

                        ev = [0]

                        def f_mm(j):
                            Wd, W3, B_Wd = WD[j % 2]
                            for d in range(2):
                                for q in range(4):
                                    gp = 4 * j + q
                                    for part in range(2):
                                        ps, pb = PS.next()
                                        for dl in range(8):
                                            if q < 3:
                                                lw = Wd[32 * q:32 * q + 32, d, part, dl, :]
                                                ru = u5r[32 * q:32 * q + 32, j, dl, :]
                                            else:
                                                lw = W3[64:128, d, part, dl, :]
                                                ru = u5r[64:128, j, dl, :]
                                            S.op("pe", lambda e, ps=ps, lw=lw, ru=ru, dl=dl: e.matmul(ps[:, :], lhsT=lw, rhs=ru, start=(dl == 0), stop=(dl == 7)),
                                                 reads=[B_Wd] + B_u5[j], writes=[pb])
                                        if ev[0] % 2 == 0:
                                            S.op("act", lambda e, ps=ps, d=d, part=part, gp=gp: e.activation(out=FS[:, d, part, gp, :], in_=ps[:, :], func=AF.Copy), reads=[pb], writes=[B_FS[d]])
                                        else:
                                            S.op("dve", lambda e, ps=ps, d=d, part=part, gp=gp: e.tensor_copy(out=FS[:, d, part, gp, :], in_=ps[:, :]), reads=[pb], writes=[B_FS[d]])
                                        ev[0] += 1

                        build_w(0)
                        for j in range(4):
                            if j + 1 < 4:
                                build_w(j + 1)
                            f_mm(j)
                    S.barrier()
                    if debug and "F" in dbg_out:
                        with ExitStack() as pd:
                            tmp = sbt(pd, "dbg_F", [128, 2 * 2 * 16 * 512], F32); tb_ = Buf()
                            S.op("dve", lambda e: e.tensor_copy(out=tmp[:], in_=FS[:].rearrange("p a b c d -> p (a b c d)")), writes=[tb_])
                            S.op("sp", lambda e: e.dma_start(out=dbg_out["F"], in_=tmp[:]), reads=[tb_], writes=[B_dbg], dma=True)
                        S.barrier()

                    with ExitStack() as pr:
                        NSEG, SL = 8, 64
                        A1 = sbt(pr, "A1", [128, 2, 2, 16], F32)
                        A2 = sbt(pr, "A2", [128, 2, 2, 16], F32)
                        Wst = sbt(pr, "Wst", [128, 2, 2, 2, 16], F32)
                        for d in range(2):
                            cs = slice(d * 16, (d + 1) * 16)
                            for part in range(2):
                                vop(lambda e, d=d, part=part, cs=cs: e.tensor_copy(out=A1[:, d, part, :], in_=Pre[:, 8, cs]))
                            vop(lambda e, d=d, cs=cs: e.tensor_scalar(out=A2[:, d, 0, :], in0=Pim[:, 8, cs], scalar1=-1.0, scalar2=None, op0=ALU.mult))
                            vop(lambda e, d=d, cs=cs: e.tensor_copy(out=A2[:, d, 1, :], in_=Pim[:, 8, cs]))
                            for part in range(2):
                                vop(lambda e, d=d, part=part, cs=cs: e.tensor_copy(out=Wst[:, d, 0, part, :], in_=Pre[:, 8, cs]))
                            vop(lambda e, d=d, cs=cs: e.tensor_copy(out=Wst[:, d, 1, 0, :], in_=Pim[:, 8, cs]))
                            vop(lambda e, d=d, cs=cs: e.tensor_scalar(out=Wst[:, d, 1, 1, :], in0=Pim[:, 8, cs], scalar1=-1.0, scalar2=None, op0=ALU.mult))
                        S.barrier()
                        for d in range(2):
                            eng_ = "dve" if d == 0 else "pool"
                            cs = slice(d * 16, (d + 1) * 16)
                            R = sbt(pr, "R%d" % d, [128, 2, 16, NSEG], F32); Rb = Buf()
                            t1 = sbt(pr, "t1%d" % d, [128, 2, 16, NSEG], F32); t1b = Buf()
                            t2 = sbt(pr, "t2%d" % d, [128, 2, 16, NSEG], F32); t2b = Buf()
                            APr = sbt(pr, "APr%d" % d, [128, 16, SL], F32)
                            APi = sbt(pr, "APi%d" % d, [128, 16, SL], F32)
                            Bap = Buf()
                            D1 = sbt(pr, "D1%d" % d, [128, 16, SL], F32); D1b = Buf()
                            D2 = sbt(pr, "D2%d" % d, [128, 16, SL], F32); D2b = Buf()
                            Et = sbt(pr, "Et%d" % d, [128, 2, 16], F32); Etb = Buf()
                            Tst = sbt(pr, "Tst%d" % d, [128, 2, 32, NSEG], F32); Tstb = Buf()
                            APiN = sbt(pr, "APiN%d" % d, [128, 2, 16, SL], F32); Bapn = Buf()
                            T1 = sbt(pr, "T1%d" % d, [128, 2, 16, SL], F32); T1b = Buf()
                            T2 = sbt(pr, "T2%d" % d, [128, 2, 16, SL], F32); T2b = Buf()
                            e1 = sbt(pr, "e1%d" % d, [128, 2, 16], F32); e1b = Buf()
                            e2 = sbt(pr, "e2%d" % d, [128, 2, 16], F32); e2b = Buf()

                            cur_eng = [eng_]

                            def X(fn, reads, writes, cur_eng=cur_eng):
                                S.op(cur_eng[0], fn, reads=reads, writes=writes)

                            cur_eng[0] = "dve"

                            X(lambda e: e.tensor_copy(out=APr[:, :, 0], in_=Pre[:, 8, cs]), [Bp], [Bap])
                            X(lambda e: e.tensor_copy(out=APi[:, :, 0], in_=Pim[:, 8, cs]), [Bp], [Bap])
                            n_ = 1
                            while n_ < SL:
                                sr = APr[:, :, n_ - 1:n_].to_broadcast([128, 16, n_])
                                si = APi[:, :, n_ - 1:n_].to_broadcast([128, 16, n_])
                                X(lambda e, n_=n_, sr=sr: e.tensor_tensor(out=D1[:, :, 0:n_], in0=APr[:, :, 0:n_], in1=sr, op=ALU.mult), [Bap], [D1b])
                                X(lambda e, n_=n_, si=si: e.tensor_tensor(out=D2[:, :, 0:n_], in0=APi[:, :, 0:n_], in1=si, op=ALU.mult), [Bap], [D2b])
                                X(lambda e, n_=n_: e.tensor_tensor(out=APr[:, :, n_:2 * n_], in0=D1[:, :, 0:n_], in1=D2[:, :, 0:n_], op=ALU.subtract), [D1b, D2b], [Bap])
                                X(lambda e, n_=n_, si=si: e.tensor_tensor(out=D1[:, :, 0:n_], in0=APr[:, :, 0:n_], in1=si, op=ALU.mult), [Bap], [D1b])
                                X(lambda e, n_=n_, sr=sr: e.tensor_tensor(out=D2[:, :, 0:n_], in0=APi[:, :, 0:n_], in1=sr, op=ALU.mult), [Bap], [D2b])
                                X(lambda e, n_=n_: e.tensor_tensor(out=APi[:, :, n_:2 * n_], in0=D1[:, :, 0:n_], in1=D2[:, :, 0:n_], op=ALU.add), [D1b, D2b], [Bap])
                                n_ *= 2
                            X(lambda e: e.tensor_scalar(out=APiN[:, 0], in0=APi[:], scalar1=-1.0, scalar2=None, op0=ALU.mult), [Bap], [Bapn])
                            X(lambda e: e.tensor_copy(out=APiN[:, 1], in_=APi[:]), [Bap], [Bapn])
                            cur_eng[0] = eng_
                            FSv = FS[:, d].rearrange("p a g (s i) -> p a g s i", s=NSEG)
                            A1b = A1[:, d].unsqueeze(3).to_broadcast([128, 2, 16, NSEG])
                            A2b = A2[:, d].unsqueeze(3).to_broadcast([128, 2, 16, NSEG])
                            FSm = FS[:, d].rearrange("p a g (s i) -> p (a g) s i", s=NSEG)
                            Wb = Wst[:, d].rearrange("p r a g -> p r (a g)").unsqueeze(3).to_broadcast([128, 2, 32, NSEG])
                            for i in range(1, SL):
                                ii = i if d == 0 else SL - 1 - i
                                pi = ii - 1 if d == 0 else ii + 1
                                X(lambda e, pi=pi: e.tensor_tensor(out=Tst[:], in0=FSm[:, :, :, pi].unsqueeze(1).to_broadcast([128, 2, 32, NSEG]), in1=Wb, op=ALU.mult), [B_FS[d]], [Tstb])
                                X(lambda e, ii=ii: e.tensor_tensor(out=FSm[:, :, :, ii], in0=FSm[:, :, :, ii], in1=Tst[:, 0], op=ALU.add), [Tstb, B_FS[d]], [B_FS[d]])
                                X(lambda e, ii=ii: e.tensor_tensor(out=FSv[:, :, :, :, ii], in0=FSv[:, :, :, :, ii], in1=Tst[:, 1].rearrange("p (a g) s -> p a g s", a=2)[:, ::-1], op=ALU.add),
                                  [Tstb, B_FS[d]], [B_FS[d]])
                            last = SL - 1 if d == 0 else 0
                            X(lambda e, last=last: e.tensor_copy(out=R[:], in_=FSv[:, :, :, :, last]), [B_FS[d]], [Rb])
                            cur_eng[0] = "dve"
                            segs = list(range(NSEG)) if d == 0 else list(range(NSEG - 1, -1, -1))
                            X(lambda e: e.tensor_copy(out=Et[:], in_=FSv[:, :, :, segs[0], last]), [B_FS[d]], [Etb])
                            APrb = APr[:].unsqueeze(1).to_broadcast([128, 2, 16, SL])
                            for si_, sg in enumerate(segs[1:]):
                                if d == 0:
                                    fs = FS[:, d, :, :, sg * SL:(sg + 1) * SL]
                                else:
                                    lo_ = sg * SL
                                    fs = FS[:, d, :, :, lo_ + SL - 1:(lo_ - 1 if lo_ > 0 else None):-1]
                                eb = Et[:].unsqueeze(3).to_broadcast([128, 2, 16, SL])
                                esw = Et[:, ::-1, :].unsqueeze(3).to_broadcast([128, 2, 16, SL])
                                X(lambda e, eb=eb: e.tensor_tensor(out=T1[:], in0=APrb, in1=eb, op=ALU.mult), [Bap, Etb], [T1b])
                                X(lambda e, esw=esw: e.tensor_tensor(out=T2[:], in0=APiN[:], in1=esw, op=ALU.mult), [Bapn, Etb], [T2b])
                                X(lambda e: e.tensor_tensor(out=T1[:], in0=T1[:], in1=T2[:], op=ALU.add), [T1b, T2b], [T1b])
                                X(lambda e, fs=fs: e.tensor_tensor(out=fs, in0=fs, in1=T1[:], op=ALU.add), [T1b, B_FS[d]], [B_FS[d]])
                                if si_ < NSEG - 2:
                                    X(lambda e, sg=sg: e.tensor_copy(out=Et[:], in_=FSv[:, :, :, sg, last]), [B_FS[d]], [Etb])
                    S.barrier()
                    if debug and "S" in dbg_out:
                        with ExitStack() as pd:
                            tmp = sbt(pd, "dbg_S", [128, 2 * 2 * 16 * 512], F32); tb_ = Buf()
                            S.op("dve", lambda e: e.tensor_copy(out=tmp[:], in_=FS[:].rearrange("p a b c d -> p (a b c d)")), writes=[tb_])
                            S.op("sp", lambda e: e.dma_start(out=dbg_out["S"], in_=tmp[:]), reads=[tb_], writes=[B_dbg], dma=True)
                        S.barrier()

                    with ExitStack() as po:
                        XB = sbt(po, "XB", [128, 2, 32, 2, 16], F32)
                        B_XB = Buf("XB")
                        for part in range(2):
                            for gl in range(2):
                                S.op("dve", lambda e, part=part, gl=gl: e.tensor_scalar(out=XB[:, part, :, gl, :], in0=BB[:, part], scalar1=maskg[:, gl:gl + 1], scalar2=None, op0=ALU.mult),
                                     reads=[Bp, B_const], writes=[B_XB])
                        CA = sbt(po, "CA", [128, 2, 9, 8, 16], F32)
                        B_CA = Buf("CA")
                        tC = sbt(po, "tC", [128, 9, 4, 16], F32); tCb = Buf()
                        tD = sbt(po, "tD", [128, 9, 4, 16], F32); tDb = Buf()
                        XC = sbt(po, "XC", [128, 9, 2, 8, 2, 16], F32)
                        B_XC = Buf("XC")
                        CCW = [(sbt(po, "Ccw%d" % i, [128, 36, 5, 32], BF16), Buf("Ccw%d" % i)) for i in range(2)]
                        KW = [(sbt(po, "Kw%d" % i, [128, 2, 8, 128], BF16), Buf("Kw%d" % i)) for i in range(2)]
                        XBp = sbt(po, "XBp", [128, 2, 4, 128], F32)
                        B_XBp = Buf("XBp")
                        S.op("pool", lambda e: e.memset(XBp[:].rearrange("p a b c -> p (a b c)"), 0.0), writes=[B_XBp])
                        for i in range(2):
                            S.op("pool", lambda e, i=i: e.memset(CCW[i][0][:].rearrange("p a b c -> p (a b c)"), 0.0), writes=[CCW[i][1]])
                        y32 = [(sbt(po, "y32_%d" % i, [128, 512], F32), Buf()) for i in range(2)]
                        yt = [(sbt(po, "yt_%d" % i, [128, 512], F32), Buf()) for i in range(2)]
                        y1s = [(sbt(po, "y1s_%d" % i, [128, 512], BF16), Buf()) for i in range(2)]

                        def build_y(j):
                            Ccw, B_Ccw = CCW[j % 2]
                            Kw, B_Kw = KW[j % 2]
                            for d in range(2):
                                cs = slice(d * 16 + 4 * j, d * 16 + 4 * j + 4)
                                ds = slice(d * 4, d * 4 + 4)
                                sr = Pre[:, 0:9, cs].unsqueeze(3).to_broadcast([128, 9, 4, 16])
                                si = Pim[:, 0:9, cs].unsqueeze(3).to_broadcast([128, 9, 4, 16])
                                xr = CC[:, 0, cs].unsqueeze(1).to_broadcast([128, 9, 4, 16])
                                xi = CC[:, 1, cs].unsqueeze(1).to_broadcast([128, 9, 4, 16])
                                S.op("dve", lambda e, xr=xr, sr=sr: e.tensor_tensor(out=tC[:], in0=xr, in1=sr, op=ALU.mult), reads=[Bp], writes=[tCb])
                                S.op("dve", lambda e, xi=xi, si=si: e.tensor_tensor(out=tD[:], in0=xi, in1=si, op=ALU.mult), reads=[Bp], writes=[tDb])
                                S.op("dve", lambda e, ds=ds: e.tensor_tensor(out=CA[:, 0, :, ds, :], in0=tC[:], in1=tD[:], op=ALU.subtract), reads=[tCb, tDb], writes=[B_CA])
                                S.op("dve", lambda e, xr=xr, si=si: e.tensor_tensor(out=tC[:], in0=xr, in1=si, op=ALU.mult), reads=[Bp], writes=[tCb])
                                S.op("dve", lambda e, xi=xi, sr=sr: e.tensor_tensor(out=tD[:], in0=xi, in1=sr, op=ALU.mult), reads=[Bp], writes=[tDb])
                                S.op("dve", lambda e, ds=ds: e.tensor_tensor(out=CA[:, 1, :, ds, :], in0=tC[:], in1=tD[:], op=ALU.add), reads=[tCb, tDb], writes=[B_CA])
                            for gl in range(2):
                                S.op("dve", lambda e, gl=gl: e.tensor_scalar(out=XC[:, :, 0, :, gl, :], in0=CA[:, 0], scalar1=maskg[:, gl:gl + 1], scalar2=None, op0=ALU.mult),
                                     reads=[B_CA, B_const], writes=[B_XC])
                                S.op("dve", lambda e, gl=gl: e.tensor_scalar(out=XC[:, :, 1, :, gl, :], in0=CA[:, 1], scalar1=maskg[:, gl:gl + 1], scalar2=-1.0, op0=ALU.mult, op1=ALU.mult),
                                     reads=[B_CA, B_const], writes=[B_XC])
                            XCv = XC[:].rearrange("p e a (d q) g f -> p (e a d) q (g f)", d=2)
                            S.op("act", lambda e, XCv=XCv, Ccw=Ccw: e.activation(out=Ccw[:, :, 0:3, :], in_=XCv[:, :, 0:3, :], func=AF.Copy), reads=[B_XC], writes=[B_Ccw])
                            S.op("act", lambda e, XCv=XCv, Ccw=Ccw: e.activation(out=Ccw[:, :, 4, :], in_=XCv[:, :, 3, :], func=AF.Copy), reads=[B_XC], writes=[B_Ccw])
                            for d in range(2):
                                for part in range(2):
                                    for q in range(4):
                                        col = d * 16 + 4 * j + q
                                        S.op("pool", lambda e, part=part, q=q, col=col: e.tensor_copy(out=XBp[:, part, q, 32 * q:32 * q + 32], in_=XB[:, part, col].rearrange("p a b -> p (a b)")),
                                             reads=[B_XB], writes=[B_XBp])
                                for k in range(8):
                                    ps, pb = PS.next()
                                    for q in range(4):
                                        for part in range(2):
                                            S.op("pe", lambda e, ps=ps, q=q, part=part, k=k, d=d: e.matmul(
                                                ps[:, 32 * q:32 * q + 32],
                                                lhsT=XBp[:, part, q, :],
                                                rhs=XC[:, k, part, d * 4 + q].rearrange("p a b -> p (a b)"),
                                                start=(part == 0), stop=(part == 1)), reads=[B_XBp, B_XC], writes=[pb])
                                    S.op("act", lambda e, ps=ps, d=d, k=k, Kw=Kw: e.activation(out=Kw[:, d, k, :], in_=ps[:, 0:128], func=AF.Copy), reads=[pb], writes=[B_Kw])

                        def mm_y(j, tb):
                            Ccw, B_Ccw = CCW[j % 2]
                            Kw, B_Kw = KW[j % 2]
                            sl = slice(tb * 512, (tb + 1) * 512)
                            ps, pb = PS.next()
                            cs_ = slice(tb * 64, (tb + 1) * 64)
                            rd = [B_Kw] + B_u5[j] + [B_Ccw] + B_FS
                            S.op("pe", lambda e: e.matmul(ps[:, :], lhsT=Kw[:, 0, 0, :], rhs=u5r[:, j, :, cs_], start=True, stop=False), reads=rd, writes=[pb])
                            for k in range(1, 8):
                                S.op("pe", lambda e, k=k: e.matmul(ps[:, k * 64:512], lhsT=Kw[:, 0, k, :], rhs=u5r[:, j, 0:8 - k, cs_], start=False, stop=False), reads=rd, writes=[pb])
                            for k in range(0, 8):
                                S.op("pe", lambda e, k=k: e.matmul(ps[:, 0:(8 - k) * 64], lhsT=Kw[:, 1, k, :], rhs=u5r[:, j, k:8, cs_], start=False, stop=False), reads=rd, writes=[pb])
                            c0g = tb * 64
                            mm = []

                            def orow(q):
                                return ps[32 * q:32 * q + 32] if q < 3 else ps[64:128]

                            def cw(e_, part, d, q):
                                i_ = (e_ * 2 + part) * 2 + d
                                return Ccw[:, i_, q, :] if q < 3 else Ccw[:, i_, 3:5, :].rearrange("p a b -> p (a b)")

                            for q in range(4):
                                gp = 4 * j + q
                                for tau in range(8):
                                    for part in range(2):
                                        lo = 1 if tb == 0 else 0
                                        mm.append((orow(q)[:, tau * 64 + lo:tau * 64 + 64], cw(tau + 1, part, 0, q), FS[:, 0, part, gp, c0g + lo - 1:c0g + 63]))
                                        hi = 63 if tb == 7 else 64
                                        mm.append((orow(q)[:, tau * 64:tau * 64 + hi], cw(8 - tau, part, 1, q), FS[:, 1, part, gp, c0g + 1:c0g + hi + 1]))
                            for i_, (o_, w_, r_) in enumerate(mm):
                                S.op("pe", lambda e, o_=o_, w_=w_, r_=r_, last=(i_ == len(mm) - 1): e.matmul(o_, lhsT=w_, rhs=r_, start=False, stop=last), reads=rd, writes=[pb])
                            y_t, yb = y32[tb % 2]
                            t_t, tb_ = yt[tb % 2]
                            o_t, ob_ = y1s[tb % 2]
                            S.op("dve", lambda e: e.scalar_tensor_tensor(out=y_t[:, :].rearrange("p (c t) -> p c t", t=8), in0=u5r[:, j, :, cs_].rearrange("p t c -> p c t"), scalar=dsk[:, j:j + 1],
                                                                         in1=ps[:, :].rearrange("p (t c) -> p c t", t=8), op0=ALU.mult, op1=ALU.add),
                                 reads=[pb, Bp] + B_u5[j], writes=[yb])
                            S.op("pool", lambda e: e.tensor_tensor(out=t_t[:, :], in0=y_t[:, :], in1=y_t[:, :], op=ALU.mult), reads=[yb], writes=[tb_])
                            S.op("pool", lambda e: e.tensor_scalar(out=t_t[:, :], in0=t_t[:, :], scalar1=0.044715, scalar2=1.0, op0=ALU.mult, op1=ALU.add), reads=[tb_], writes=[tb_])
                            S.op("pool", lambda e: e.tensor_tensor(out=t_t[:, :], in0=t_t[:, :], in1=y_t[:, :], op=ALU.mult), reads=[yb, tb_], writes=[tb_])
                            S.op("act", lambda e: e.activation(out=t_t[:, :], in_=t_t[:, :], func=AF.Sigmoid, scale=2.0 * math.sqrt(2.0 / math.pi)), reads=[tb_], writes=[tb_])
                            S.op("dve", lambda e: e.tensor_tensor(out=o_t[:, :], in0=y_t[:, :], in1=t_t[:, :], op=ALU.mult), reads=[yb, tb_], writes=[ob_])
                            S.op("sp", lambda e: e.dma_start(out=y1_d[j * 128:(j + 1) * 128, sl], in_=o_t[:, :]), reads=[ob_], writes=[B_Y1[j][tb]], dma=True)

                        build_y(0)
                        for j in range(4):
                            for tb in range(8):
                                mm_y(j, tb)
                                if tb == 1 and j + 1 < 4:
                                    build_y(j + 1)
                    S.barrier()
                    with ExitStack() as pg:
                        wg = sbt(pg, "wg", [128, 4, 512], BF16); wgb = Buf()
                        S.op("pool", lambda e: e.dma_start(out=wg[:], in_=w_glu[l].rearrange("(k p) c -> p k c", p=128)), writes=[wgb], dma=True)
                        sg = [(sbt(pg, "sg%d" % i, [128, 512], F32), Buf()) for i in range(2)]
                        so = [(sbt(pg, "so%d" % i, [128, 4, 512], BF16), Buf()) for i in range(2)]
                        yl = [(sbt(pg, "yl%d" % i, [128, 4, 512], BF16), Buf()) for i in range(2)]
                        for tb in range(8):
                            sl = slice(tb * 512, (tb + 1) * 512)
                            so_t, sob = so[tb % 2]
                            yl_t, ylb = yl[tb % 2]
                            S.op("sp", lambda e, yl_t=yl_t, sl=sl: e.dma_start(out=yl_t[:], in_=y1_d[:, sl].rearrange("(m p) t -> p m t", p=128)), reads=[B_Y1[k][tb] for k in range(4)], writes=[ylb], dma=True)
                            for m in range(4):
                                ps, pb = PS.next()
                                for k in range(4):
                                    S.op("pe", lambda e, ps=ps, m=m, k=k, yl_t=yl_t: e.matmul(ps[:, :], lhsT=wg[:, k, m * 128:(m + 1) * 128], rhs=yl_t[:, k, :], start=(k == 0), stop=(k == 3)),
                                         reads=[wgb, ylb], writes=[pb])
                                sg_t, sgb = sg[m % 2]
                                S.op("act", lambda e, ps=ps, sg_t=sg_t: e.activation(out=sg_t[:, :], in_=ps[:, :], func=AF.Sigmoid), reads=[pb], writes=[sgb])
                                S.op("dve", lambda e, sg_t=sg_t, so_t=so_t, m=m, yl_t=yl_t: e.tensor_tensor(out=so_t[:, m, :], in0=yl_t[:, m, :], in1=sg_t[:, :], op=ALU.mult),
                                     reads=[sgb, ylb], writes=[sob])
                            S.op("sp", lambda e, so_t=so_t, sl=sl: e.dma_start(out=ys5_d[:, sl].rearrange("(m p) t -> p m t", p=128), in_=so_t[:]), reads=[sob], writes=[B_ys5[tb]], dma=True)
                    S.barrier()
            S.barrier()
            dbg_dump("ys5", ys5_d, B_ys5)
            dbg_dump("q", q_d, [B_q])
            if stop_after == "s5":
                break

            with ExitStack() as pt:
                wm = sbt(pt, "wm", [128, 48, 256], BF16); wmb = Buf()
                for i in range(6):
                    S.op("pool", lambda e, i=i: e.dma_start(out=wm[:, i * 8:(i + 1) * 8, :], in_=wmask_d[:, i * 8:(i + 1) * 8, :]), writes=[wmb], dma=True)
                qz = [[(sbt(pt, "q%d_%d" % (i, hh), [128, L], BF16), Buf()) for hh in range(2)] for i in range(2)]
                ks = [(sbt(pt, "k%d" % i, [128, L + 2 * KPAD], BF16), Buf()) for i in range(2)]
                vs = [(sbt(pt, "v%d" % i, [128, L + 2 * KPAD], BF16), Buf()) for i in range(2)]
                for i in range(2):
                    for (t_, b_) in (ks[i], vs[i]):
                        S.op("pool", lambda e, t_=t_: e.memset(t_[:, 0:KPAD], 0.0), writes=[b_])
                        S.op("pool", lambda e, t_=t_: e.memset(t_[:, KPAD + L:], 0.0), writes=[b_])
                    for hh in range(2):
                        t_, b_ = qz[i][hh]
                        zr = slice(64, 128) if hh == 0 else slice(0, 64)
                        S.op("pool", lambda e, t_=t_, zr=zr: e.memset(t_[zr, :], 0.0), writes=[b_])
                acc = sbt(pt, "acc", [65, 2, L], F32)
                B_acc = [Buf(), Buf()]
                VPs = [(sbt(pt, "Vp%d" % i, [128, 48, 2, 65], BF16), Buf()) for i in range(2)]
                Es = [(sbt(pt, "E%d" % i, [128, 512], BF16), Buf()) for i in range(5)]
                Ps_ = [(sbt(pt, "P%d" % i, [128, 512], BF16), Buf()) for i in range(5)]
                ER_, PR_ = Rot(Es), Rot(Ps_)
                rds = [(sbt(pt, "rd%d" % i, [65, 512], F32), Buf()) for i in range(2)]
                yo = [(sbt(pt, "yo%d" % i, [64, L], BF16), Buf()) for i in range(2)]
                PSS = Rot(psum[0:4])
                PSO = Rot(psum[4:6])
                mcnt = [0]
                vcnt = [0]

                def load_hp(hp):
                    k_t, kb_ = ks[hp % 2]
                    v_t, vb_ = vs[hp % 2]
                    rows = slice(hp * 128, (hp + 1) * 128)
                    for hh in range(2):
                        t_, b_ = qz[hp % 2][hh]
                        S.op("sp", lambda e, t_=t_, hh=hh: e.dma_start(out=t_[64 * hh:64 * hh + 64, :], in_=q_d[hp * 128 + 64 * hh:hp * 128 + 64 * hh + 64, :]), reads=[B_q], writes=[b_], dma=True)
                    S.op("sp", lambda e: e.dma_start(out=k_t[:, KPAD:KPAD + L], in_=k_d[rows, :]), reads=[B_k], writes=[kb_], dma=True)
                    S.op("sp", lambda e: e.dma_start(out=v_t[:, KPAD:KPAD + L], in_=v_d[rows, :]), reads=[B_v], writes=[vb_], dma=True)

                def normalise(hp):
                    for hh in range(2):
                        yo_t, yob = yo[hh]
                        for tb in range(8):
                            sl = slice(tb * 512, (tb + 1) * 512)
                            ps, pb = PSS.next()
                            S.op("pe", lambda e: e.matmul(ps[0:64, :], lhsT=ones_f[64:65, 0:64], rhs=acc[64:65, hh, sl], start=True, stop=True), reads=[B_acc[hh], B_const], writes=[pb])
                            rd_t, rdb = rds[tb % 2]
                            S.op("act", lambda e: e.activation(out=rd_t[0:64, :], in_=ps[0:64, :], func=AF.Ln), reads=[pb], writes=[rdb])
                            S.op("act", lambda e: e.activation(out=rd_t[0:64, :], in_=rd_t[0:64, :], func=AF.Exp, scale=-1.0), reads=[rdb], writes=[rdb])
                            S.op("pool", lambda e: e.tensor_tensor(out=yo_t[:, sl], in0=acc[0:64, hh, sl], in1=rd_t[0:64, :], op=ALU.mult), reads=[rdb, B_acc[hh]], writes=[yob])
                        r0 = hp * 128 + hh * 64
                        S.op("sp", lambda e: e.dma_start(out=yattn_d[r0:r0 + 64, :], in_=yo_t[:, :]), reads=[yob], writes=[B_yattn], dma=True)

                jobs = []
                jn = 0
                for hp in range(8):
                    k_t, kb_ = ks[hp % 2]
                    v_t, vb_ = vs[hp % 2]
                    for pi, D in enumerate(PATTERNS):
                        M = L // D
                        nqb = M // 128
                        nkt = nqb + 1
                        ntl = D * nkt
                        Vp, B_Vp = VPs[jn % 2]
                        jn += 1
                        job = {"vp_groups": [], "units": [], "pre": None, "post": None}
                        if pi == 0:
                            job["pre"] = (lambda hp=hp: load_hp(hp + 1)) if hp + 1 < 8 else None
                        if pi == len(PATTERNS) - 1:
                            job["post"] = (lambda hp=hp: normalise(hp))

                        def vp_init(Vp=Vp, B_Vp=B_Vp, nqb=nqb, nkt=nkt, ntl=ntl):
                            S.op("pool", lambda e: e.memset(Vp[:, 0:ntl, :, 64:65], 1.0), writes=[B_Vp])
                            S.op("pool", lambda e: e.memset(Vp[0:64, 0:ntl:nkt, :, 64:65], 0.0), writes=[B_Vp])
                            S.op("pool", lambda e: e.memset(Vp[64:128, nqb:ntl:nkt, :, 64:65], 0.0), writes=[B_Vp])
                        job["vp_groups"].append(vp_init)
                        for t0_ in range(0, ntl, 8):
                            def vp_group(t0_=t0_, Vp=Vp, B_Vp=B_Vp, D=D, nkt=nkt, ntl=ntl, v_t=v_t, vb_=vb_):
                                nj = min(8, ntl - t0_)
                                pbt, pbb = PSB.next()
                                for jo in range(nj):
                                    t_ = t0_ + jo
                                    r, jj = t_ // nkt, t_ % nkt
                                    c0 = KPAD + r + D * (128 * jj - 64)
                                    S.op("pe", lambda e, c0=c0, jo=jo: e.transpose(out=pbt[:, jo * 128:(jo + 1) * 128], in_=v_t[:, c0:c0 + 127 * D + 1:D], identity=ident_b[:, :]),
                                         reads=[vb_, B_const], writes=[pbb])
                                src = pbt[:, 0:nj * 128].rearrange("p (j h e) -> p j h e", h=2, e=64)
                                vi = vcnt[0]
                                vcnt[0] += 1
                                if vi % 2 == 0:
                                    S.op("act", lambda e: e.activation(out=Vp[:, t0_:t0_ + nj, :, 0:64], in_=src, func=AF.Copy), reads=[pbb], writes=[B_Vp])
                                else:
                                    S.op("dve", lambda e: e.tensor_copy(out=Vp[:, t0_:t0_ + nj, :, 0:64], in_=src), reads=[pbb], writes=[B_Vp])
                            job["vp_groups"].append(vp_group)

                        gsz = min(4, nqb)
                        for r in range(D):
                            for hh in range(2):
                                q_t, qb_ = qz[hp % 2][hh]
                                widx = pi * 16 + hp * 2 + hh
                                ostate = {"po": None, "pob": None}
                                for j0 in range(0, nkt, 2):
                                    tiles = [j for j in (j0, j0 + 1) if j < nkt]
                                    ctx = {}

                                    def stage1(tiles=tiles, j0=j0, r=r, widx=widx, ctx=ctx, D=D, nqb=nqb, q_t=q_t, qb_=qb_, k_t=k_t, kb_=kb_):
                                        ps, pb = PSS.next()
                                        for j in tiles:
                                            qlo, qhi = max(j - 1, 0), min(j, nqb - 1)
                                            nq = (qhi - qlo + 1) * 128
                                            qc0 = r + D * 128 * qlo
                                            kc0 = KPAD + r + D * (128 * j - 64)
                                            so = 256 * (j - j0) + (128 if j == 0 else 0)
                                            S.op("pe", lambda e, kc0=kc0, qc0=qc0, nq=nq, so=so: e.matmul(
                                                ps[:, so:so + nq], lhsT=k_t[:, kc0:kc0 + 127 * D + 1:D], rhs=q_t[:, qc0:qc0 + (nq - 1) * D + 1:D], start=True, stop=True),
                                                reads=[kb_, qb_], writes=[pb])
                                        c_lo = 128 if j0 == 0 else 0
                                        c_hi = 256 * (len(tiles) - 1) + (128 if tiles[-1] == nqb else 256)
                                        e_t, eb = ER_.next()
                                        p_t, pbf = PR_.next()
                                        ctx["p"] = (p_t, pbf)
                                        S.op("act", lambda e: e.activation(out=e_t[:, c_lo:c_hi], in_=ps[:, c_lo:c_hi], func=AF.Exp), reads=[pb], writes=[eb])
                                        me = "pool" if mcnt[0] % 3 == 2 else "dve"
                                        mcnt[0] += 1
                                        if c_lo == 0 and c_hi == 512:
                                            S.op(me, lambda e: e.tensor_tensor(out=p_t[:, :].rearrange("p (a b) -> p a b", a=2), in0=e_t[:, :].rearrange("p (a b) -> p a b", a=2),
                                                                               in1=wm[:, widx:widx + 1, :].to_broadcast([128, 2, 256]), op=ALU.mult),
                                                 reads=[eb, wmb], writes=[pbf])
                                        else:
                                            for j in tiles:
                                                so = 256 * (j - j0)
                                                lo_ = 128 if j == 0 else 0
                                                hi_ = 128 if j == nqb else 256
                                                S.op(me, lambda e, so=so, lo_=lo_, hi_=hi_: e.tensor_tensor(out=p_t[:, so + lo_:so + hi_], in0=e_t[:, so + lo_:so + hi_],
                                                                                                           in1=wm[:, widx, lo_:hi_], op=ALU.mult),
                                                     reads=[eb, wmb], writes=[pbf])

                                    def stage2(tiles=tiles, j0=j0, r=r, hh=hh, ctx=ctx, D=D, nqb=nqb, nkt=nkt, gsz=gsz, Vp=Vp, B_Vp=B_Vp, ostate=ostate, pi=pi):
                                        p_t, pbf = ctx["p"]
                                        for j in tiles:
                                            so = 256 * (j - j0)
                                            parts = []
                                            if j >= 1:
                                                parts.append((j - 1, so))
                                            if j <= nqb - 1:
                                                parts.append((j, so + 128))
                                            if len(parts) == 2 and (j - 1) // gsz == j // gsz:
                                                groups = [((j - 1), so, 256)]
                                            else:
                                                groups = [(qb, c_, 128) for (qb, c_) in parts]
                                            for (qb, c_, n_) in groups:
                                                if qb % gsz == 0 and c_ == so + 128:
                                                    ostate["po"], ostate["pob"] = PSO.next()
                                                    first = True
                                                else:
                                                    first = False
                                                po_, pob = ostate["po"], ostate["pob"]
                                                last = (n_ == 128 and c_ == so and qb % gsz == gsz - 1)
                                                oc = (qb % gsz) * 128
                                                S.op("pe", lambda e, oc=oc, n_=n_, j=j, c_=c_, first=first, last=last, po_=po_: e.matmul(
                                                    po_[0:65, oc:oc + n_], lhsT=Vp[:, r * nkt + j, hh, :], rhs=p_t[:, c_:c_ + n_], start=first, stop=last, skip_group_check=True),
                                                    reads=[B_Vp, pbf], writes=[pob])
                                                if last:
                                                    g0 = (qb // gsz) * gsz
                                                    t0 = r + D * 128 * g0
                                                    nn = gsz * 128
                                                    dst = acc[0:65, hh, t0:t0 + (nn - 1) * D + 1:D]
                                                    if pi == 0:
                                                        S.op("act", lambda e, dst=dst, po_=po_, nn=nn: e.activation(out=dst, in_=po_[0:65, 0:nn], func=AF.Copy), reads=[pob], writes=[B_acc[hh]])
                                                    else:
                                                        S.op("dve", lambda e, dst=dst, po_=po_, nn=nn: e.tensor_tensor(out=dst, in0=po_[0:65, 0:nn], in1=dst, op=ALU.add), reads=[pob, B_acc[hh]], writes=[B_acc[hh]])

                                    job["units"].append((stage1, stage2))
                        jobs.append(job)

                LAG = 4
                load_hp(0)
                for g in jobs[0]["vp_groups"]:
                    g()
                flat = []
                for ji, job in enumerate(jobs):
                    for ui, (s1, s2) in enumerate(job["units"]):
                        flat.append((ji, ui, s1, s2))
                pending = {}
                for fi in range(len(flat) + LAG):
                    if fi < len(flat):
                        ji, ui, s1, s2 = flat[fi]
                        job = jobs[ji]
                        if ui == 0 and job["pre"] is not None:
                            job["pre"]()
                        if ui == LAG and ji + 1 < len(jobs):
                            pending[ji] = list(jobs[ji + 1]["vp_groups"])
                        s1()
                        if pending.get(ji):
                            pending[ji].pop(0)()
                        if ui == len(job["units"]) - 1:
                            while pending.get(ji):
                                pending[ji].pop(0)()
                    if fi >= LAG:
                        ji, ui, s1, s2 = flat[fi - LAG]
                        s2()
                        if ui == len(jobs[ji]["units"]) - 1 and jobs[ji]["post"] is not None:
                            jobs[ji]["post"]()
            S.barrier()
            dbg_dump("yattn", yattn_d, [B_yattn])
            if stop_after == "attn":
                break

            with ExitStack() as pc:
                wgt = sbt(pc, "c_wg", [128, 8, 3072], BF16)
                wb5 = sbt(pc, "c_wb5", [128, 4, DM], BF16)
                wbp = sbt(pc, "c_wbp", [128, 4, DM], BF16)
                wba = sbt(pc, "c_wba", [128, 8, DM], BF16)
                wo = sbt(pc, "c_wo", [128, 8, DM], BF16)
                Bwg = [[Buf() for _ in range(2)] for _ in range(3)]
                Bwb = [[Buf() for _ in range(2)] for _ in range(3)]
                Bwo = Buf("c1wo")
                wbrs = (wb5, wbp, wba)
                wsrc = (w_bs5, w_bpool, w_battn)
                def issue_c1_weights():
                    for hf in range(2):
                        for b in range(3):
                            c0 = 4096 + b * 1024 + hf * 512
                            S.op("pool", lambda e, b=b, hf=hf, c0=c0: e.dma_start(out=wgt[:, :, b * 1024 + hf * 512:b * 1024 + (hf + 1) * 512],
                                                                                in_=w_in[l].rearrange("(k p) c -> p k c", p=128)[:, :, c0:c0 + 512]), writes=[Bwg[b][hf]], dma=True)
                            S.op("pool", lambda e, b=b, hf=hf: e.dma_start(out=wbrs[b][:, :, hf * 512:(hf + 1) * 512],
                                                                          in_=wsrc[b][l].rearrange("(k p) c -> p k c", p=128)[:, :, hf * 512:(hf + 1) * 512]), writes=[Bwb[b][hf]], dma=True)
                    for k in range(8):
                        S.op("pool", lambda e, k=k: e.dma_start(out=wo[:, k, :], in_=w_out[l, k * 128:(k + 1) * 128, :]), writes=[Bwo], dma=True)
                with ExitStack() as pp:
                    up = sbt(pp, "upool", [128, 4, L + 16], BF16)
                    B_up = [Buf() for _ in range(4)]
                    for g in range(4):
                        S.op("dve", lambda e, g=g: e.memset(up[:, g, 0:8], 0.0), writes=[B_up[g]])
                        S.op("dve", lambda e, g=g: e.memset(up[:, g, L + 8:L + 16], 0.0), writes=[B_up[g]])
                        S.op("sp", lambda e, g=g: e.dma_start(out=up[:, g, 8:8 + L], in_=upool_d[g * 128:(g + 1) * 128, :]), reads=[B_upd], writes=[B_up[g]], dma=True)
                    pw = sbt(pp, "p_w", [128, 4, 128], BF16); pwb = Buf()
                    psc = sbt(pp, "p_sc", [128, 4], F32)
                    S.op("pool", lambda e: e.dma_start(out=pw[:], in_=pool_w[l].rearrange("g c d -> c g d")), writes=[pwb], dma=True)
                    S.op("sp", lambda e: e.dma_start(out=psc[:], in_=pool_scale[:, l, :]), writes=[pwb], dma=True)
                    issue_c1_weights()
                    ta = sbt(pp, "p_ta", [128, L + 16], F32); tab = Buf()
                    tb2 = sbt(pp, "p_tb", [128, L + 16], F32); tbb = Buf()
                    pl = sbt(pp, "p_pl", [128, L], BF16); plb = Buf()
                    stp = [(sbt(pp, "p_st%d" % i, [128, 512], BF16), Buf()) for i in range(2)]
                    O = 8
                    for g in range(4):
                        win = POOL_WINS[g]
                        u = up[:, g, :]
                        ub = B_up[g]
                        S.op("dve", lambda e, u=u: e.tensor_tensor(out=ta[:, O - 7:O + L + 8], in0=u[:, O - 7:O + L + 8], in1=u[:, O - 8:O + L + 7], op=ALU.add), reads=[ub], writes=[tab])
                        cur, curb, oth, othb = ta, tab, tb2, tbb
                        lo, hi = -7, L + 8
                        sh = 1
                        w_ = 2
                        while w_ < win:
                            lo2, hi2 = lo + sh, hi - sh
                            S.op("dve", lambda e, cur=cur, oth=oth, lo2=lo2, hi2=hi2, sh=sh: e.tensor_tensor(out=oth[:, O + lo2:O + hi2], in0=cur[:, O + lo2 + sh:O + hi2 + sh],
                                                                                                          in1=cur[:, O + lo2 - sh:O + hi2 - sh], op=ALU.add), reads=[curb], writes=[othb])
                            cur, curb, oth, othb = oth, othb, cur, curb
                            lo, hi = lo2, hi2
                            sh *= 2
                            w_ *= 2
                        S.op("dve", lambda e, cur=cur, u=u, win=win: e.scalar_tensor_tensor(out=pl[:, :], in0=cur[:, O:O + L], scalar=1.0 / win, in1=u[:, O:O + L], op0=ALU.mult, op1=ALU.subtract),
                             reads=[curb, ub], writes=[plb])
                        left = win // 2
                        right = win - 1 - left
                        for t in range(L):
                            hi_ = min(t + right + 1, L)
                            lo_ = max(t - left, 0)
                            if hi_ - lo_ != win:
                                S.op("dve", lambda e, cur=cur, u=u, t=t, c_=float(hi_ - lo_): e.scalar_tensor_tensor(out=pl[:, t:t + 1], in0=cur[:, O + t:O + t + 1], scalar=1.0 / c_,
                                                                                                               in1=u[:, O + t:O + t + 1], op0=ALU.mult, op1=ALU.subtract),
                                     reads=[curb, ub], writes=[plb])
                        for tt in range(8):
                            sl = slice(tt * 512, (tt + 1) * 512)
                            ps, pb = PS.next()
                            S.op("pe", lambda e, ps=ps, g=g, sl=sl: e.matmul(ps[:, :], lhsT=pw[:, g, :], rhs=pl[:, sl], start=True, stop=True), reads=[pwb, plb], writes=[pb])
                            st_t, stb = stp[tt % 2]
                            S.op("act", lambda e, ps=ps, st_t=st_t, g=g: e.activation(out=st_t[:, :], in_=ps[:, :], func=AF.Copy, scale=psc[:, g:g + 1]), reads=[pb, pwb], writes=[stb])
                            S.op("sp", lambda e, st_t=st_t, g=g, sl=sl: e.dma_start(out=ypool_d[g * 128:(g + 1) * 128, sl], in_=st_t[:, :]), reads=[stb], writes=[B_ypool[tt]], dma=True)
                S.barrier()
                INB = [dict(xn=(sbt(pc, "c_xn%d" % i, [128, 8, 512], BF16), Buf()), y5=(sbt(pc, "c_y5%d" % i, [128, 4, 512], BF16), Buf()),
                            yp=(sbt(pc, "c_yp%d" % i, [128, 4, 512], BF16), Buf()), ya=(sbt(pc, "c_ya%d" % i, [128, 8, 512], BF16), Buf())) for i in range(2)]

                def c1_loads(tt):
                    sl = slice(tt * 512, (tt + 1) * 512)
                    ib = INB[tt % 2]
                    S.op("sp", lambda e: e.dma_start(out=ib["xn"][0][:], in_=r_kpt(xn_d)[:, :, sl]), reads=[B_xn[2 * tt], B_xn[2 * tt + 1]], writes=[ib["xn"][1]], dma=True)
                    S.op("sp", lambda e: e.dma_start(out=ib["y5"][0][:], in_=r_kpt(ys5_d)[:, :, sl]), reads=[B_ys5[tt]], writes=[ib["y5"][1]], dma=True)
                    S.op("sp", lambda e: e.dma_start(out=ib["yp"][0][:], in_=r_kpt(ypool_d)[:, :, sl]), reads=[B_ypool[tt]], writes=[ib["yp"][1]], dma=True)
                    S.op("sp", lambda e: e.dma_start(out=ib["ya"][0][:], in_=r_kpt(yattn_d)[:, :, sl]), reads=[B_yattn], writes=[ib["ya"][1]], dma=True)

                c1_loads(0)
                h_t = sbt(pc, "c_h", [128, 8, 512], F32); hb = Buf()
                mg_t = sbt(pc, "c_mg", [128, 8, 512], BF16); mgb_all = Buf(); mgb = [mgb_all] * 8
                gts = [(sbt(pc, "c_g%d" % i, [128, 512], F32), Buf()) for i in range(2)]
                GR = Rot(gts)
                acs = [(sbt(pc, "c_ac%d" % i, [128, 512], F32), Buf()) for i in range(2)]
                hn_t = sbt(pc, "c_hn", [128, 8, 512], BF16); hnb = Buf()
                rs_t = sbt(pc, "c_rs", [128, 512], F32); rsb = Buf()
                for tt in range(8):
                    sl = slice(tt * 512, (tt + 1) * 512)
                    if tt + 1 < 8:
                        c1_loads(tt + 1)
                    ib = INB[tt % 2]
                    xn_t, xnb = ib["xn"]
                    y5_t, y5b = ib["y5"]
                    yp_t, ypb = ib["yp"]
                    ya_t, yab = ib["ya"]
                    S.op("sp", lambda e, sl=sl: e.dma_start(out=h_t[:], in_=r_kpt(hT)[:, :, sl]), reads=[B_hT[2 * tt], B_hT[2 * tt + 1]], writes=[hb], dma=True)
                    branches = ((wb5, y5_t, y5b, 4), (wbp, yp_t, ypb, 4), (wba, ya_t, yab, 8))
                    for m in range(8):
                        ms = slice(m * 128, (m + 1) * 128)
                        ac_t, acb = acs[m % 2]
                        for b in range(3):
                            wbr, y_t, yb_, nk = branches[b]
                            ps, pb = PS.next()
                            for k in range(8):
                                S.op("pe", lambda e, ps=ps, b=b, m=m, k=k: e.matmul(ps[:, :], lhsT=wgt[:, k, b * 1024 + m * 128:b * 1024 + (m + 1) * 128], rhs=xn_t[:, k, :], start=(k == 0), stop=(k == 7)),
                                     reads=[Bwg[b][m // 4], xnb], writes=[pb])
                            g_t, gb = GR.next()
                            S.op("act", lambda e, ps=ps, g_t=g_t: e.activation(out=g_t[:, :], in_=ps[:, :], func=AF.Sigmoid), reads=[pb], writes=[gb])
                            ps2, pb2 = PS.next()
                            for k in range(nk):
                                S.op("pe", lambda e, ps2=ps2, wbr=wbr, y_t=y_t, ms=ms, k=k, nk=nk: e.matmul(ps2[:, :], lhsT=wbr[:, k, ms], rhs=y_t[:, k, :], start=(k == 0), stop=(k == nk - 1)),
                                     reads=[Bwb[b][m // 4], yb_], writes=[pb2])
                            if b == 0:
                                S.op("dve", lambda e, ps2=ps2, g_t=g_t, ac_t=ac_t: e.tensor_tensor(out=ac_t[:, :], in0=ps2[:, :], in1=g_t[:, :], op=ALU.mult), reads=[pb2, gb], writes=[acb])
                            else:
                                S.op("dve", lambda e, ps2=ps2, g_t=g_t: e.tensor_tensor(out=g_t[:, :], in0=ps2[:, :], in1=g_t[:, :], op=ALU.mult), reads=[pb2, gb], writes=[gb])
                                if b == 1:
                                    S.op("pool", lambda e, g_t=g_t, ac_t=ac_t: e.tensor_tensor(out=ac_t[:, :], in0=ac_t[:, :], in1=g_t[:, :], op=ALU.add), reads=[gb, acb], writes=[acb])
                                else:
                                    S.op("pool", lambda e, g_t=g_t, ac_t=ac_t, m=m: e.tensor_tensor(out=mg_t[:, m, :], in0=ac_t[:, :], in1=g_t[:, :], op=ALU.add), reads=[gb, acb], writes=[mgb[m]])
                    for m in range(8):
                        ms = slice(m * 128, (m + 1) * 128)
                        ps, pb = PS.next()
                        for k in range(8):
                            S.op("pe", lambda e, ps=ps, ms=ms, k=k: e.matmul(ps[:, :], lhsT=wo[:, k, ms], rhs=mg_t[:, k, :], start=(k == 0), stop=(k == 7)), reads=[Bwo, mgb[k]], writes=[pb])
                        S.op("dve", lambda e, ps=ps, m=m: e.tensor_tensor(out=h_t[:, m, :], in0=ps[:, :], in1=h_t[:, m, :], op=ALU.add), reads=[pb, hb], writes=[hb])
                    S.op("sp", lambda e, sl=sl: e.dma_start(out=r_kpt(hT)[:, :, sl], in_=h_t[:]), reads=[hb], writes=[B_hT[2 * tt], B_hT[2 * tt + 1]], dma=True)
                    emit_norm(h_t, hb, gmlp[:, l, :], hn_t, hnb, mg_t, mgb_all, rs_t, rsb, 512)
                    S.op("sp", lambda e, sl=sl: e.dma_start(out=r_kpt(xn_d)[:, :, sl], in_=hn_t[:]), reads=[hnb], writes=[B_xn[2 * tt], B_xn[2 * tt + 1]], dma=True)
            S.barrier()
            dbg_dump("h1", hT, B_hT)
            dbg_dump("hn", xn_d, B_xn)
            if stop_after == "c1":
                break

            with ExitStack() as pf:
                wu = sbt(pf, "f_wu", [128, 8, 4096], BF16)
                wd = sbt(pf, "f_wd", [128, 32, DM], BF16)
                Bwu = [Buf() for _ in range(8)]
                Bwd = [Buf() for _ in range(32)]
                for cb in range(8):
                    S.op("pool", lambda e, cb=cb: e.dma_start(out=wu[:, :, cb * 512:(cb + 1) * 512], in_=w_up[l].rearrange("(k p) c -> p k c", p=128)[:, :, cb * 512:(cb + 1) * 512]), writes=[Bwu[cb]], dma=True)
                for f in range(32):
                    S.op("pool", lambda e, f=f: e.dma_start(out=wd[:, f, :], in_=w_down[l, f * 128:(f + 1) * 128, :]), writes=[Bwd[f]], dma=True)
                NT = 512
                hn_t = sbt(pf, "f_hn", [128, 8, NT], BF16); hnb = Buf()
                h_t = sbt(pf, "f_h", [128, 8, NT], F32); hb = Buf()
                act_t = sbt(pf, "f_act", [128, 32, NT], BF16); actb = [Buf() for _ in range(32)]
                rl = [(sbt(pf, "f_rl%d" % i, [128, NT], F32), Buf()) for i in range(2)]
                RL = Rot(rl)
                sq_t = sbt(pf, "f_sq", [128, 8, NT], BF16); sqb = Buf()
                rs_t = sbt(pf, "f_rs", [128, NT], F32); rsb = Buf()
                xo_t = sbt(pf, "f_xo", [128, 8, NT], BF16); xob = Buf()
                S.op("sp", lambda e: e.dma_start(out=hn_t[:], in_=r_kpt(xn_d)[:, :, 0:NT]), reads=[B_xn[0], B_xn[1]], writes=[hnb], dma=True)
                for tt in range(8):
                    sl = slice(tt * NT, (tt + 1) * NT)
                    S.op("sp", lambda e, sl=sl: e.dma_start(out=h_t[:], in_=r_kpt(hT)[:, :, sl]), reads=[B_hT[2 * tt], B_hT[2 * tt + 1]], writes=[hb], dma=True)
                    for f in range(32):
                        ps, pb = PS.next()
                        for k in range(8):
                            S.op("pe", lambda e, ps=ps, f=f, k=k: e.matmul(ps[:, :NT], lhsT=wu[:, k, f * 128:(f + 1) * 128], rhs=hn_t[:, k, :], start=(k == 0), stop=(k == 7)), reads=[Bwu[f // 4], hnb], writes=[pb])
                        r_t, rb = RL.next()
                        S.op("act", lambda e, ps=ps, r_t=r_t: e.activation(out=r_t[:, :], in_=ps[:, :NT], func=AF.Relu), reads=[pb], writes=[rb])
                        me = "dve" if f % 2 == 0 else "pool"
                        S.op(me, lambda e, r_t=r_t, f=f: e.tensor_tensor(out=act_t[:, f, :], in0=r_t[:, :], in1=r_t[:, :], op=ALU.mult), reads=[rb], writes=[actb[f]])
                    if tt + 1 < 8:
                        S.op("sp", lambda e, tt=tt: e.dma_start(out=hn_t[:], in_=r_kpt(xn_d)[:, :, (tt + 1) * NT:(tt + 2) * NT]), reads=[B_xn[2 * tt + 2], B_xn[2 * tt + 3]], writes=[hnb], dma=True)
                    for m in range(8):
                        ms = slice(m * 128, (m + 1) * 128)
                        ps, pb = PS.next()
                        for f in range(32):
                            S.op("pe", lambda e, ps=ps, ms=ms, f=f: e.matmul(ps[:, :NT], lhsT=wd[:, f, ms], rhs=act_t[:, f, :], start=(f == 0), stop=(f == 31)), reads=[Bwd[f], actb[f]], writes=[pb])
                        S.op("dve", lambda e, ps=ps, m=m: e.tensor_tensor(out=h_t[:, m, :], in0=ps[:, :NT], in1=h_t[:, m, :], op=ALU.add), reads=[pb, hb], writes=[hb])
                    if l < DEPTH - 1:
                        S.op("sp", lambda e, sl=sl: e.dma_start(out=r_kpt(hT)[:, :, sl], in_=h_t[:]), reads=[hb], writes=[B_hT[2 * tt], B_hT[2 * tt + 1]], dma=True)
                        emit_norm(h_t, hb, gmix[:, l + 1, :], xo_t, xob, sq_t, sqb, rs_t, rsb, NT)
                        S.op("sp", lambda e, sl=sl: e.dma_start(out=r_kpt(xn_d)[:, :, sl], in_=xo_t[:]), reads=[xob], writes=[B_xn[2 * tt], B_xn[2 * tt + 1]], dma=True)
                    else:
                        emit_norm(h_t, hb, gfin[:, :], h_t, hb, sq_t, sqb, rs_t, rsb, NT)
                        S.op("sp", lambda e, sl=sl: e.dma_start(out=r_kpt(outT)[:, :, sl], in_=h_t[:]), reads=[hb], writes=[B_out], dma=True)
            S.barrier()

        counts = S.emit(es)
    return nc, counts, len(S.ops)


def _state_layout(a, has_p):
    l_, d_ = a.shape[0], a.shape[1]
    if has_p:
        a = a.reshape(l_, d_, 16, 2, 64, 16).transpose(3, 4, 0, 1, 2, 5).reshape(128, l_, 32, 16)
    else:
        a = a.reshape(l_, d_, 16, 2, 64).transpose(3, 4, 0, 1, 2).reshape(128, l_, 32)
    return np.ascontiguousarray(a)


def _attn_masks():
    slopes = np.exp2(-8.0 * np.arange(1, 17, dtype=np.float64) / 16)
    a = np.arange(128)[:, None]
    b = np.arange(128)[None, :]
    relA = np.abs(a - 64 - b)
    relB = np.abs(a + 64 - b)
    out = np.zeros((128, 48, 256), np.float32)
    for pi, D in enumerate(PATTERNS):
        for h in range(16):
            wa = np.where(relA <= 64, np.exp(-slopes[h] * D * relA), 0.0)
            wb = np.where(relB <= 64, np.exp(-slopes[h] * D * relB), 0.0)
            out[:, pi * 16 + h, 0:128] = wb
            out[:, pi * 16 + h, 128:256] = wa
    return out


def make_shared_inputs(inp):
    f = lambda k: np.ascontiguousarray(np.asarray(inp[k], dtype=np.float32))
    sh = {}
    sh["w_in"] = f("w_in")
    sh["g_mix"] = np.ascontiguousarray(f("norm_mix").reshape(DEPTH, 8, 128).transpose(2, 0, 1))
    sh["g_mlp"] = np.ascontiguousarray(f("norm_mlp").reshape(DEPTH, 8, 128).transpose(2, 0, 1))
    sh["g_fin"] = np.ascontiguousarray(f("norm_final").reshape(8, 128).transpose(1, 0))
    sh["lam_re"] = _state_layout(f("s5_lam_re"), False)
    sh["lam_im"] = _state_layout(f("s5_lam_im"), False)
    ld = f("s5_log_dt")
    sh["log_dt"] = _state_layout(np.broadcast_to(ld[..., None], ld.shape + (64,)), False)
    sh["b_re"] = _state_layout(f("s5_b_re"), True)
    sh["b_im"] = _state_layout(f("s5_b_im"), True)
    sh["c_re"] = _state_layout(f("s5_c_re").transpose(0, 1, 2, 4, 3), True)
    sh["c_im"] = _state_layout(f("s5_c_im").transpose(0, 1, 2, 4, 3), True)
    sh["s5_d"] = np.ascontiguousarray(f("s5_d").reshape(DEPTH, 4, 128).transpose(2, 0, 1))
    sh["w_glu"] = f("s5_w_glu")
    sh["pool_w"] = f("pool_w")
    sh["pool_scale"] = np.ascontiguousarray(f("pool_scale").reshape(DEPTH, 4, 128).transpose(2, 0, 1))
    sh["w_bs5"] = f("w_branch_s5")
    sh["w_bpool"] = f("w_branch_pool")
    sh["w_battn"] = f("w_branch_attn")
    sh["w_out"] = f("w_out")
    sh["w_up"] = f("w_up")
    sh["w_down"] = f("w_down")
    sh["ident"] = np.eye(128, dtype=np.float32)
    mg = np.zeros((128, 2), np.float32)
    mg[:64, 0] = 1.0
    mg[64:, 1] = 1.0
    sh["maskg"] = mg
    sh["wmask"] = _attn_masks()
    return sh


_CACHE = {}


def kernel(**inputs):
    x = np.asarray(inputs["x"], dtype=np.float32)
    if "nc" not in _CACHE:
        _CACHE["nc"] = build_program()[0]
    nc = _CACHE["nc"]
    sh = make_shared_inputs(inputs)
    in_maps = []
    for b in range(NCORES):
        m = dict(sh)
        m["xT"] = np.ascontiguousarray(x[b].T)
        in_maps.append(m)
    res = run_bass_kernel_spmd(nc, in_maps, core_ids=list(range(NCORES)))
    out = np.stack([np.ascontiguousarray(np.asarray(r["outT"]).T) for r in res.results]).astype(np.float32)
    return out
```

```python
import math
import numpy as np
import concourse.bass as bass
import concourse.mybir as mybir
from concourse.bass_utils import run_bass_kernel_spmd
from contextlib import ExitStack

F32 = mybir.dt.float32
BF16 = mybir.dt.bfloat16
AF = mybir.ActivationFunctionType
ALU = mybir.AluOpType

L = 4096
DM = 1024
DEPTH = 2
NCORES = 8
EPS = 1e-6
TWO_PI = 2.0 * math.pi
PATTERNS = (1, 4, 16)
KPAD = 1024
POOL_WINS = (2, 4, 8, 16)


class Buf:
    __slots__ = ("w", "r", "name", "excl")

    def __init__(self, name="", excl=False):
        self.w = None
        self.r = {}
        self.name = name
        self.excl = excl


class _Rec:
    def __init__(self):
        self.call = None

    def __getattr__(self, name):
        def f(*a, **k):
            self.call = (name, a, k)
            return None
        return f


class Sched:
    COMPUTE = ("pe", "act", "dve", "pool")

    def __init__(self, nc, n_dma_slots=14):
        self.nc = nc
        self.ops = []
        self.n_dma_slots = n_dma_slots

    def op(self, stream, fn, reads=(), writes=(), dma=False):
        rd = tuple(b for b in reads if not b.excl)
        wr = tuple(writes) + tuple(b for b in reads if b.excl)
        rec = _Rec()
        fn(rec)
        self.ops.append((stream, rec.call, rd, wr, dma))

    def barrier(self):
        self.ops.append(("barrier", None, (), (), False))

    def emit(self, es):
        nc = self.nc
        eng = {"pe": nc.tensor, "act": nc.scalar, "dve": nc.vector, "pool": nc.gpsimd, "sp": nc.sync}
        ops = self.ops
        n = len(ops)
        deps = [None] * n
        signal = [False] * n
        pos_in_stream = [0] * n
        cnt = {}
        last_compute = {}
        for i, (st, fn, reads, writes, dma) in enumerate(ops):
            if st == "barrier":
                d = set(last_compute.values())
                deps[i] = d
                for j in d:
                    signal[j] = True
                continue
            cnt[st] = cnt.get(st, 0) + 1
            pos_in_stream[i] = cnt[st]
            d = set()
            for b in reads:
                if b.w is not None:
                    d.add(b.w)
            for b in writes:
                if b.w is not None:
                    d.add(b.w)
                for j in b.r.values():
                    d.add(j)
            dd = set()
            for j in d:
                sj = ops[j][0]
                jd = ops[j][4]
                if sj == st and not jd and not dma:
                    if st == "pe":
                        continue
                dd.add(j)
            deps[i] = dd
            for j in dd:
                signal[j] = True
            for b in reads:
                b.r[(st, i) if dma else st] = i
            for b in writes:
                b.w = i
                b.r = {}
            if not dma:
                last_compute[st] = i
        sems = {}
        for st in self.COMPUTE:
            sems[st] = es.enter_context(nc.semaphore("s_" + st))
        dma_sems = [es.enter_context(nc.semaphore("s_dma%d" % k)) for k in range(self.n_dma_slots)]
        dma_slot_val = [0] * self.n_dma_slots
        dma_next = 0
        dma_next_sw = 0
        n_hw = 9
        sig_val = [None] * n
        counts = {st: 0 for st in sems}
        waited = {}

        def do_wait(st, sem, key, val):
            k = (st, key)
            if waited.get(k, 0) >= val:
                return
            waited[k] = val
            eng[st].wait_ge(sem, val)

        streams_all = ("pe", "act", "dve", "pool", "sp")
        for i, (st, fn, reads, writes, dma) in enumerate(ops):
            if st == "barrier":
                for s2 in streams_all:
                    for j in deps[i]:
                        sem, val, key = sig_val[j]
                        do_wait(s2, sem, key, val)
                    for slot in range(self.n_dma_slots):
                        if dma_slot_val[slot] > 0:
                            do_wait(s2, dma_sems[slot], ("dma", slot), dma_slot_val[slot])
                continue
            for j in sorted(deps[i]):
                sem, val, key = sig_val[j]
                do_wait(st, sem, key, val)
            if dma:
                if st == "sp":
                    slot = dma_next
                    dma_next = (dma_next + 1) % n_hw
                else:
                    slot = n_hw + dma_next_sw
                    dma_next_sw = (dma_next_sw + 1) % (self.n_dma_slots - n_hw)
                if dma_slot_val[slot] > 0:
                    do_wait(st, dma_sems[slot], ("dma", slot), dma_slot_val[slot])
                ins = getattr(eng[st], fn[0])(*fn[1], **fn[2])
                dma_slot_val[slot] += 16
                ins.then_inc(dma_sems[slot], 16)
                sig_val[i] = (dma_sems[slot], dma_slot_val[slot], ("dma", slot))
            else:
                ins = getattr(eng[st], fn[0])(*fn[1], **fn[2])
                if signal[i]:
                    counts[st] += 1
                    ins.then_inc(sems[st], 1)
                    sig_val[i] = (sems[st], counts[st], st)
        for slot in range(self.n_dma_slots):
            if dma_slot_val[slot] > 0:
                do_wait("sp", dma_sems[slot], ("dma", slot), dma_slot_val[slot])
        return counts


class Rot:
    def __init__(self, items):
        self.items = items
        self.i = 0

    def next(self):
        it = self.items[self.i]
        self.i = (self.i + 1) % len(self.items)
        return it


def build_program(debug=None, stop_after=None):
    nc = bass.Bass("TRN2", target_bir_lowering=False)
    S = Sched(nc)
    dbg_out = {}

    def din(name, shape, dt=F32):
        return nc.dram_tensor(name, list(shape), dt, kind="ExternalInput").ap()

    def dscr(name, shape, dt):
        return nc.dram_tensor(name, list(shape), dt, kind="Internal").ap()

    xT = din("xT", [DM, L])
    w_in = din("w_in", [DEPTH, DM, 7168])
    g_mix = din("g_mix", [128, DEPTH, 8])
    g_mlp = din("g_mlp", [128, DEPTH, 8])
    g_fin = din("g_fin", [128, 8])
    lam_re = din("lam_re", [128, DEPTH, 32])
    lam_im = din("lam_im", [128, DEPTH, 32])
    log_dt = din("log_dt", [128, DEPTH, 32])
    b_re = din("b_re", [128, DEPTH, 32, 16])
    b_im = din("b_im", [128, DEPTH, 32, 16])
    c_re = din("c_re", [128, DEPTH, 32, 16])
    c_im = din("c_im", [128, DEPTH, 32, 16])
    s5_d = din("s5_d", [128, DEPTH, 4])
    w_glu = din("w_glu", [DEPTH, 512, 512])
    pool_w = din("pool_w", [DEPTH, 4, 128, 128])
    pool_scale = din("pool_scale", [128, DEPTH, 4])
    w_bs5 = din("w_bs5", [DEPTH, 512, DM])
    w_bpool = din("w_bpool", [DEPTH, 512, DM])
    w_battn = din("w_battn", [DEPTH, DM, DM])
    w_out = din("w_out", [DEPTH, DM, DM])
    w_up = din("w_up", [DEPTH, DM, 4096])
    w_down = din("w_down", [DEPTH, 4096, DM])
    ident_d = din("ident", [128, 128])
    maskg_d = din("maskg", [128, 2])
    wmask_d = din("wmask", [128, 48, 256])
    outT = nc.dram_tensor("outT", [DM, L], F32, kind="ExternalOutput").ap()
    if debug:
        for name, shape in debug.items():
            dbg_out[name] = nc.dram_tensor("dbg_" + name, list(shape), F32, kind="ExternalOutput").ap()

    hT = dscr("hT", [DM, L], F32)
    xn_d = dscr("xn_d", [DM, L], BF16)
    q_d = dscr("q_d", [DM, L], BF16)
    k_d = dscr("k_d", [DM, L], BF16)
    v_d = dscr("v_d", [DM, L], BF16)
    ys5_d = dscr("ys5_d", [512, L], BF16)
    upool_d = dscr("upool_d", [512, L], BF16)
    B_upd = Buf("upool_d")
    y1_d = dscr("y1_d", [512, L], BF16)
    ypool_d = dscr("ypool_d", [512, L], BF16)
    yattn_d = dscr("yattn_d", [DM, L], BF16)
    B_hT = [Buf("hT%d" % i) for i in range(16)]
    B_xn = [Buf("xn%d" % i) for i in range(16)]
    B_q, B_k, B_v = Buf("q"), Buf("k"), Buf("v")
    B_ys5 = [Buf() for _ in range(8)]
    B_ypool = [Buf() for _ in range(8)]
    B_yattn = Buf("yattn")
    B_out = Buf("out")
    B_dbg = Buf("dbg")

    def r_kpt(ap):
        return ap.rearrange("(k p) t -> p k t", p=128)

    with ExitStack() as es:
        uid = [0]

        def sbt(stack, name, shape, dt):
            uid[0] += 1
            return stack.enter_context(nc.sbuf_tensor("sb%d_%s" % (uid[0], name), list(shape), dt))

        ones_bf = sbt(es, "ones_bf", [128, 128], BF16)
        ones_f = sbt(es, "ones_f", [128, 64], F32)
        ident_f = sbt(es, "ident_f", [128, 128], F32)
        ident_b = sbt(es, "ident_b", [128, 128], BF16)
        maskg = sbt(es, "maskg", [128, 2], F32)
        gmix = sbt(es, "gmix", [128, DEPTH, 8], F32)
        gmlp = sbt(es, "gmlp", [128, DEPTH, 8], F32)
        gfin = sbt(es, "gfin", [128, 8], F32)
        B_const = Buf("const")
        psum = []
        for i in range(6):
            t = es.enter_context(nc.psum_tensor("ps%d" % i, [128, 512], F32))
            psum.append((t, Buf("ps%d" % i, excl=True)))
        psbfs = [(es.enter_context(nc.psum_tensor("psbf%d" % i, [128, 1024], BF16)), Buf("psbf%d" % i, excl=True)) for i in range(2)]
        PSB = Rot(psbfs)
        PS = Rot(psum)

        eps_t = sbt(es, "eps_t", [128, 1], F32)
        S.op("dve", lambda e: e.memset(eps_t[:], EPS), writes=[B_const])
        S.op("dve", lambda e: e.memset(ones_bf[:], 1.0), writes=[B_const])
        S.op("dve", lambda e: e.memset(ones_f[:], 1.0), writes=[B_const])
        S.op("sp", lambda e: e.dma_start(out=ident_f[:], in_=ident_d), writes=[B_const], dma=True)
        S.op("pool", lambda e: e.dma_start(out=ident_b[:], in_=ident_d), writes=[B_const], dma=True)
        S.op("sp", lambda e: e.dma_start(out=maskg[:], in_=maskg_d), writes=[B_const], dma=True)
        S.op("sp", lambda e: e.dma_start(out=gmix[:], in_=g_mix), writes=[B_const], dma=True)
        S.op("sp", lambda e: e.dma_start(out=gmlp[:], in_=g_mlp), writes=[B_const], dma=True)
        S.op("sp", lambda e: e.dma_start(out=gfin[:], in_=g_fin), writes=[B_const], dma=True)
        S.barrier()

        def dbg_store(name, src_ap, src_buf, eng="sp"):
            if debug and name in dbg_out:
                S.op("sp", lambda e: e.dma_start(out=dbg_out[name], in_=src_ap), reads=[src_buf], writes=[B_dbg], dma=True)

        def dbg_dump(name, src, bufs):
            if debug and name in dbg_out:
                S.barrier()
                if src.dtype == F32:
                    S.op("sp", lambda e: e.dma_start(out=dbg_out[name], in_=src), reads=list(bufs), writes=[B_dbg], dma=True)
                else:
                    S.op("pool", lambda e: e.dma_start(out=dbg_out[name].rearrange("r (a b) -> r a b", b=2048), in_=src.rearrange("r (a b) -> r a b", b=2048)),
                         reads=list(bufs), writes=[B_dbg], dma=True)
                S.barrier()

        def emit_norm(h_t, hb, g_ap, out_t, ob, sq_t, sqb, rs_t, rsb, nt):
            S.op("act", lambda e: e.activation(out=sq_t[:, :, :nt], in_=h_t[:, :, :nt], func=AF.Square), reads=[hb], writes=[sqb])
            ps, pb = PS.next()
            for k in range(8):
                S.op("pe", lambda e, k=k: e.matmul(ps[:, :nt], lhsT=ones_bf[:, :], rhs=sq_t[:, k, :nt], start=(k == 0), stop=(k == 7)),
                     reads=[sqb], writes=[pb])
            S.op("act", lambda e: e.activation(out=rs_t[:, :nt], in_=ps[:, :nt], func=AF.Ln, scale=1.0 / DM, bias=eps_t[:, 0:1]), reads=[pb], writes=[rsb])
            S.op("act", lambda e: e.activation(out=rs_t[:, :nt], in_=rs_t[:, :nt], func=AF.Exp, scale=-0.5), reads=[rsb], writes=[rsb])
            for k in range(8):
                S.op("dve", lambda e, k=k: e.scalar_tensor_tensor(out=out_t[:, k, :nt], in0=h_t[:, k, :nt], scalar=g_ap[:, k:k + 1], in1=rs_t[:, :nt],
                                                                   op0=ALU.mult, op1=ALU.mult), reads=[hb, rsb, B_const], writes=[ob])


        for l in range(DEPTH):
            with ExitStack() as ph_mix:
                u5r = sbt(ph_mix, "u5r", [128, 4, 8, 512], BF16)
                B_u5 = [[Buf() for _ in range(8)] for _ in range(4)]
                with ExitStack() as ph:
                    with ExitStack() as pa:
                        wA = sbt(pa, "a_w", [128, 8, 4096], BF16)
                        B_wA = [Buf() for _ in range(8)]
                        xts = [(sbt(pa, "a_xn%d" % i, [128, 8, 512], BF16), Buf()) for i in range(2)]
                        if l == 0:
                            hts0 = [(sbt(pa, "a_h%d" % i, [128, 8, 512], F32), Buf()) for i in range(2)]
                            sq0 = sbt(pa, "a_sq", [128, 8, 512], BF16); sq0b = Buf()
                            rs0 = sbt(pa, "a_rs", [128, 512], F32); rs0b = Buf()

                        def prep_load(tt):
                            sl_ = slice(tt * 512, (tt + 1) * 512)
                            if l == 0:
                                h_t, hb = hts0[tt % 2]
                                S.op("sp", lambda e: e.dma_start(out=h_t[:], in_=r_kpt(xT)[:, :, sl_]), writes=[hb], dma=True)
                                S.op("sp", lambda e: e.dma_start(out=r_kpt(hT)[:, :, sl_], in_=h_t[:]), reads=[hb], writes=[B_hT[2 * tt], B_hT[2 * tt + 1]], dma=True)
                            else:
                                x_t, xb_ = xts[tt % 2]
                                S.op("sp", lambda e: e.dma_start(out=x_t[:], in_=r_kpt(xn_d)[:, :, sl_]), reads=[B_xn[2 * tt], B_xn[2 * tt + 1]], writes=[xb_], dma=True)

                        def prep_norm(tt, stage):
                            if l != 0:
                                return
                            sl_ = slice(tt * 512, (tt + 1) * 512)
                            h_t, hb = hts0[tt % 2]
                            x_t, xb_ = xts[tt % 2]
                            if stage == 0:
                                S.op("act", lambda e: e.activation(out=sq0[:], in_=h_t[:], func=AF.Square), reads=[hb], writes=[sq0b])
                                return
                            ps, pb = PS.next()
                            for k in range(8):
                                S.op("pe", lambda e, k=k: e.matmul(ps[:, :], lhsT=ones_bf[:, :], rhs=sq0[:, k, :], start=(k == 0), stop=(k == 7)), reads=[sq0b], writes=[pb])
                            S.op("act", lambda e: e.activation(out=rs0[:, :], in_=ps[:, :], func=AF.Ln, scale=1.0 / DM, bias=eps_t[:, 0:1]), reads=[pb, B_const], writes=[rs0b])
                            S.op("act", lambda e: e.activation(out=rs0[:, :], in_=rs0[:, :], func=AF.Exp, scale=-0.5), reads=[rs0b], writes=[rs0b])
                            for k in range(8):
                                S.op("dve", lambda e, k=k: e.scalar_tensor_tensor(out=x_t[:, k, :], in0=h_t[:, k, :], scalar=gmix[:, 0, k:k + 1], in1=rs0[:, :], op0=ALU.mult, op1=ALU.mult),
                                     reads=[hb, rs0b, B_const], writes=[xb_])
                            S.op("sp", lambda e: e.dma_start(out=r_kpt(xn_d)[:, :, sl_], in_=x_t[:]), reads=[xb_], writes=[B_xn[2 * tt], B_xn[2 * tt + 1]], dma=True)

                        prep_load(0)
                        prep_norm(0, 0)
                        prep_norm(0, 1)
                        for cb in range(8):
                            S.op("pool", lambda e, cb=cb: e.dma_start(out=wA[:, :, cb * 512:(cb + 1) * 512], in_=w_in[l].rearrange("(k p) c -> p k c", p=128)[:, :, cb * 512:(cb + 1) * 512]),
                                 writes=[B_wA[cb]], dma=True)
                        sts = [(sbt(pa, "a_st%d" % i, [128, 4, 512], BF16), Buf()) for i in range(3)]
                        STG = Rot(sts)
                        ev = 0
                        for tt in range(8):
                            sl = slice(tt * 512, (tt + 1) * 512)
                            x_t, xb_ = xts[tt % 2]
                            if tt + 1 < 8:
                                prep_load(tt + 1)
                            for cb in range(8):
                                if tt + 1 < 8 and cb == 2:
                                    prep_norm(tt + 1, 0)
                                if tt + 1 < 8 and cb == 4:
                                    prep_norm(tt + 1, 1)
                                if cb >= 1:
                                    st_t, stb = STG.next()
                                for m in range(4):
                                    ps, pb = PS.next()
                                    for k in range(8):
                                        S.op("pe", lambda e, ps=ps, cb=cb, m=m, k=k, x_t=x_t: e.matmul(ps[:, :], lhsT=wA[:, k, cb * 512 + m * 128:cb * 512 + (m + 1) * 128], rhs=x_t[:, k, :],
                                                                                                    start=(k == 0), stop=(k == 7)),
                                             reads=[B_wA[cb], xb_], writes=[pb])
                                    if cb == 0:
                                        dst, db = u5r[:, m, :, tt * 64:(tt + 1) * 64], B_u5[m][tt]
                                        src = ps[:, :].rearrange("p (c t) -> p t c", t=8)
                                    else:
                                        dst, db = st_t[:, m, :], stb
                                        src = ps[:, :]
                                    sc = 0.125 if cb in (2, 3) else 1.0
                                    if ev % 2 == 0:
                                        S.op("act", lambda e, dst=dst, src=src, sc=sc: e.activation(out=dst, in_=src, func=AF.Copy, scale=sc), reads=[pb], writes=[db])
                                    else:
                                        S.op("dve", lambda e, dst=dst, src=src, sc=sc: e.tensor_scalar(out=dst, in0=src, scalar1=sc, scalar2=None, op0=ALU.mult), reads=[pb], writes=[db])
                                    ev += 1
                                if cb == 1:
                                    S.op("sp", lambda e, sl=sl, st_t=st_t: e.dma_start(out=upool_d[:, sl].rearrange("(m p) t -> p m t", p=128), in_=st_t[:]),
                                         reads=[stb], writes=[B_upd], dma=True)
                                if cb >= 2:
                                    dd, dbuf = ((q_d, B_q), (k_d, B_k), (v_d, B_v))[(cb - 2) // 2]
                                    r0 = ((cb - 2) % 2) * 512
                                    S.op("sp", lambda e, dd=dd, r0=r0, sl=sl, st_t=st_t: e.dma_start(out=dd[r0:r0 + 512, sl].rearrange("(m p) t -> p m t", p=128), in_=st_t[:]),
                                         reads=[stb], writes=[dbuf], dma=True)
                    S.barrier()
                    if debug and "u5" in dbg_out:
                        with ExitStack() as pd:
                            tmp = sbt(pd, "dbg_u5", [128, 4, L], F32); tb_ = Buf()
                            S.op("dve", lambda e: e.tensor_copy(out=tmp[:].rearrange("p j (c t) -> p j t c", t=8), in_=u5r[:]), writes=[tb_])
                            S.op("sp", lambda e: e.dma_start(out=dbg_out["u5"].rearrange("(j p) t -> p j t", p=128), in_=tmp[:]), reads=[tb_], writes=[B_dbg], dma=True)
                        S.barrier()


                with ExitStack() as p5:
                    FS = sbt(p5, "FS", [128, 2, 2, 16, 512], BF16)
                    B_FS = [Buf("FSf"), Buf("FSb")]
                    B_Y1 = [[Buf() for _ in range(8)] for _ in range(4)]
                    prm = sbt(p5, "prm", [128, 16, 32], F32)
                    Bp = Buf("prm")
                    Pre = sbt(p5, "Pre", [128, 9, 32], F32)
                    Pim = sbt(p5, "Pim", [128, 9, 32], F32)
                    BB = sbt(p5, "BB", [128, 2, 32, 16], F32)
                    CC = sbt(p5, "CC", [128, 2, 32, 16], F32)
                    dsk = sbt(p5, "dsk", [128, 4], F32)
                    tmpA = sbt(p5, "tmpA", [128, 32, 16], F32); tAb = Buf()
                    tmpB = sbt(p5, "tmpB", [128, 32, 16], F32); tBb = Buf()
                    LRE, LIM, DT, ER, TH, T1, T2, SN, CS, NR, NI, DEN, C0R, C0I, AM1, NAI = range(16)

                    def P_(i):
                        return prm[:, i, :]

                    def vop(fn, reads=(Bp,), writes=(Bp,), eng="dve"):
                        S.op(eng, fn, reads=list(reads), writes=list(writes))

                    S.op("sp", lambda e: e.dma_start(out=prm[:, LRE, :], in_=lam_re[:, l, :]), writes=[Bp], dma=True)
                    S.op("sp", lambda e: e.dma_start(out=prm[:, LIM, :], in_=lam_im[:, l, :]), writes=[Bp], dma=True)
                    S.op("sp", lambda e: e.dma_start(out=prm[:, DT, :], in_=log_dt[:, l, :]), writes=[Bp], dma=True)
                    S.op("sp", lambda e: e.dma_start(out=BB[:, 0], in_=b_re[:, l]), writes=[Bp], dma=True)
                    S.op("sp", lambda e: e.dma_start(out=BB[:, 1], in_=b_im[:, l]), writes=[Bp], dma=True)
                    S.op("sp", lambda e: e.dma_start(out=CC[:, 0], in_=c_re[:, l]), writes=[Bp], dma=True)
                    S.op("sp", lambda e: e.dma_start(out=CC[:, 1], in_=c_im[:, l]), writes=[Bp], dma=True)
                    S.op("sp", lambda e: e.dma_start(out=dsk[:], in_=s5_d[:, l, :]), writes=[Bp], dma=True)
                    S.barrier()
                    S.op("act", lambda e: e.activation(out=P_(DT), in_=P_(DT), func=AF.Exp), reads=[Bp], writes=[Bp])
                    vop(lambda e: e.tensor_tensor(out=P_(ER), in0=P_(LRE), in1=P_(DT), op=ALU.mult))
                    S.op("act", lambda e: e.activation(out=P_(ER), in_=P_(ER), func=AF.Exp), reads=[Bp], writes=[Bp])
                    vop(lambda e: e.tensor_tensor(out=P_(TH), in0=P_(LIM), in1=P_(DT), op=ALU.mult))
                    prm_i = sbt(p5, "prm_i", [128, 32], mybir.dt.int32)
                    for (off, dst) in ((math.pi, SN), (1.5 * math.pi, CS)):
                        vop(lambda e, off=off: e.tensor_scalar(out=P_(T1), in0=P_(TH), scalar1=off, scalar2=1.0 / TWO_PI, op0=ALU.add, op1=ALU.mult))
                        vop(lambda e: e.tensor_copy(out=prm_i[:, :], in_=P_(T1)))
                        vop(lambda e: e.tensor_copy(out=P_(T2), in_=prm_i[:, :]))
                        vop(lambda e: e.tensor_tensor(out=P_(T2), in0=P_(T1), in1=P_(T2), op=ALU.subtract))
                        vop(lambda e: e.tensor_scalar(out=P_(T1), in0=P_(T2), scalar1=0.0, scalar2=None, op0=ALU.is_lt))
                        vop(lambda e: e.tensor_tensor(out=P_(T2), in0=P_(T2), in1=P_(T1), op=ALU.add))
                        vop(lambda e: e.tensor_scalar(out=P_(T2), in0=P_(T2), scalar1=TWO_PI, scalar2=-math.pi, op0=ALU.mult, op1=ALU.add))
                        vop(lambda e: e.tensor_scalar(out=P_(T2), in0=P_(T2), scalar1=math.pi, scalar2=-math.pi, op0=ALU.min, op1=ALU.max))
                        S.op("act", lambda e, dst=dst: e.activation(out=P_(dst), in_=P_(T2), func=AF.Sin), reads=[Bp], writes=[Bp])
                    vop(lambda e: e.memset(Pre[:, 0, :], 1.0))
                    vop(lambda e: e.memset(Pim[:, 0, :], 0.0))
                    vop(lambda e: e.tensor_tensor(out=Pre[:, 1, :], in0=P_(ER), in1=P_(CS), op=ALU.mult))
                    vop(lambda e: e.tensor_tensor(out=Pim[:, 1, :], in0=P_(ER), in1=P_(SN), op=ALU.mult))
                    for k in range(2, 9):
                        vop(lambda e, k=k: e.tensor_tensor(out=P_(T1), in0=Pre[:, k - 1, :], in1=Pre[:, 1, :], op=ALU.mult))
                        vop(lambda e, k=k: e.tensor_tensor(out=P_(T2), in0=Pim[:, k - 1, :], in1=Pim[:, 1, :], op=ALU.mult))
                        vop(lambda e, k=k: e.tensor_tensor(out=Pre[:, k, :], in0=P_(T1), in1=P_(T2), op=ALU.subtract))
                        vop(lambda e, k=k: e.tensor_tensor(out=P_(T1), in0=Pre[:, k - 1, :], in1=Pim[:, 1, :], op=ALU.mult))
                        vop(lambda e, k=k: e.tensor_tensor(out=P_(T2), in0=Pim[:, k - 1, :], in1=Pre[:, 1, :], op=ALU.mult))
                        vop(lambda e, k=k: e.tensor_tensor(out=Pim[:, k, :], in0=P_(T1), in1=P_(T2), op=ALU.add))
                    vop(lambda e: e.tensor_scalar(out=P_(AM1), in0=Pre[:, 1, :], scalar1=-1.0, scalar2=None, op0=ALU.add))
                    vop(lambda e: e.tensor_tensor(out=P_(T1), in0=P_(AM1), in1=P_(LRE), op=ALU.mult))
                    vop(lambda e: e.tensor_tensor(out=P_(T2), in0=Pim[:, 1, :], in1=P_(LIM), op=ALU.mult))
                    vop(lambda e: e.tensor_tensor(out=P_(NR), in0=P_(T1), in1=P_(T2), op=ALU.add))
                    vop(lambda e: e.tensor_tensor(out=P_(T1), in0=Pim[:, 1, :], in1=P_(LRE), op=ALU.mult))
                    vop(lambda e: e.tensor_tensor(out=P_(T2), in0=P_(AM1), in1=P_(LIM), op=ALU.mult))
                    vop(lambda e: e.tensor_tensor(out=P_(NI), in0=P_(T1), in1=P_(T2), op=ALU.subtract))
                    vop(lambda e: e.tensor_tensor(out=P_(T1), in0=P_(LRE), in1=P_(LRE), op=ALU.mult))
                    vop(lambda e: e.tensor_tensor(out=P_(T2), in0=P_(LIM), in1=P_(LIM), op=ALU.mult))
                    vop(lambda e: e.tensor_tensor(out=P_(DEN), in0=P_(T1), in1=P_(T2), op=ALU.add))
                    vop(lambda e: e.reciprocal(out=P_(DEN), in_=P_(DEN)))
                    vop(lambda e: e.tensor_tensor(out=P_(C0R), in0=P_(NR), in1=P_(DEN), op=ALU.mult))
                    vop(lambda e: e.tensor_tensor(out=P_(C0I), in0=P_(NI), in1=P_(DEN), op=ALU.mult))

                    def bc(ap32):
                        n_ = ap32.shape[1]
                        return ap32.unsqueeze(2).to_broadcast([128, n_, 16])

                    def cmul(dst_re, dst_im, x_re, x_im, s_re, s_im, n_):
                        vop(lambda e: e.tensor_tensor(out=tmpA[:, :n_, :], in0=x_re, in1=bc(s_re), op=ALU.mult), reads=[Bp], writes=[tAb])
                        vop(lambda e: e.tensor_tensor(out=tmpB[:, :n_, :], in0=x_im, in1=bc(s_im), op=ALU.mult), reads=[Bp], writes=[tBb])
                        vop(lambda e: e.tensor_tensor(out=dst_re, in0=tmpA[:, :n_, :], in1=tmpB[:, :n_, :], op=ALU.subtract), reads=[tAb, tBb], writes=[Bp])
                        vop(lambda e: e.tensor_tensor(out=tmpA[:, :n_, :], in0=x_re, in1=bc(s_im), op=ALU.mult), reads=[Bp], writes=[tAb])
                        vop(lambda e: e.tensor_tensor(out=tmpB[:, :n_, :], in0=x_im, in1=bc(s_re), op=ALU.mult), reads=[Bp], writes=[tBb])
                        vop(lambda e: e.tensor_tensor(out=dst_im, in0=tmpA[:, :n_, :], in1=tmpB[:, :n_, :], op=ALU.add), reads=[tAb, tBb], writes=[Bp])

                    with ExitStack() as pbt_:
                        BBt = sbt(pbt_, "BBt", [128, 2, 32, 16], F32)
                        cmul(BBt[:, 0], BBt[:, 1], BB[:, 0], BB[:, 1], P_(C0R), P_(C0I), 32)
                        vop(lambda e: e.tensor_copy(out=BB[:], in_=BBt[:]))
                    S.barrier()

                    with ExitStack() as pw_:
                        WD = [(sbt(pw_, "Wd%d" % i, [128, 2, 2, 8, 128], BF16), sbt(pw_, "W3%d" % i, [128, 2, 2, 8, 128], BF16), Buf("Wd%d" % i)) for i in range(2)]
                        Md = sbt(pw_, "Md", [128, 2, 8, 8, 16], F32)
                        B_Md = Buf("Md")
                        Xd = sbt(pw_, "Xd", [128, 2, 8, 8, 2, 16], F32)
                        B_Xd = Buf("Xd")
                        tA4 = tmpA[:].rearrange("p (a b) c -> p a b c", a=8)
                        tB4 = tmpB[:].rearrange("p (a b) c -> p a b c", a=8)

                        def build_w(j):
                            Wd, W3, B_Wd = WD[j % 2]
                            for d in range(2):
                                cs = slice(d * 16 + 4 * j, d * 16 + 4 * j + 4)
                                ds = slice(d * 4, d * 4 + 4)
                                esl = slice(7, None, -1) if d == 0 else slice(0, 8)
                                sr = Pre[:, esl, cs].unsqueeze(3).to_broadcast([128, 8, 4, 16])
                                si = Pim[:, esl, cs].unsqueeze(3).to_broadcast([128, 8, 4, 16])
                                xr = BB[:, 0, cs].unsqueeze(1).to_broadcast([128, 8, 4, 16])
                                xi = BB[:, 1, cs].unsqueeze(1).to_broadcast([128, 8, 4, 16])
                                S.op("dve", lambda e, xr=xr, sr=sr: e.tensor_tensor(out=tA4, in0=xr, in1=sr, op=ALU.mult), reads=[Bp], writes=[tAb])
                                S.op("dve", lambda e, xi=xi, si=si: e.tensor_tensor(out=tB4, in0=xi, in1=si, op=ALU.mult), reads=[Bp], writes=[tBb])
                                S.op("dve", lambda e, ds=ds: e.tensor_tensor(out=Md[:, 0, :, ds, :], in0=tA4, in1=tB4, op=ALU.subtract), reads=[tAb, tBb], writes=[B_Md])
                                S.op("dve", lambda e, xr=xr, si=si: e.tensor_tensor(out=tA4, in0=xr, in1=si, op=ALU.mult), reads=[Bp], writes=[tAb])
                                S.op("dve", lambda e, xi=xi, sr=sr: e.tensor_tensor(out=tB4, in0=xi, in1=sr, op=ALU.mult), reads=[Bp], writes=[tBb])
                                S.op("dve", lambda e, ds=ds: e.tensor_tensor(out=Md[:, 1, :, ds, :], in0=tA4, in1=tB4, op=ALU.add), reads=[tAb, tBb], writes=[B_Md])
                            for part in range(2):
                                for gl in range(2):
                                    S.op("dve", lambda e, part=part, gl=gl: e.tensor_scalar(out=Xd[:, part, :, :, gl, :], in0=Md[:, part], scalar1=maskg[:, gl:gl + 1], scalar2=None, op0=ALU.mult),
                                         reads=[B_Md, B_const], writes=[B_Xd])
                            for dl in range(8):
                                for part in range(2):
                                    for d in range(2):
                                        ps, pb = PS.next()
                                        S.op("pe", lambda e, ps=ps, part=part, d=d, dl=dl: e.transpose(out=ps[:, 0:128], in_=Xd[:, part, dl, d * 4:d * 4 + 4].rearrange("p a b c -> p (a b c)"), identity=ident_f[:, :]),
                                             reads=[B_Xd, B_const], writes=[pb])
                                        S.op("act", lambda e, ps=ps, d=d, part=part, dl=dl, Wd=Wd: e.activation(out=Wd[:, d, part, dl, :], in_=ps[:, 0:128], func=AF.Copy), reads=[pb], writes=[B_Wd])
                                        S.op("dve", lambda e, ps=ps, d=d, part=part, dl=dl, W3=W3: e.tensor_copy(out=W3[64:128, d, part, dl, :], in_=ps[64:128, 0:128]), reads=[pb], writes=[B_Wd])
                            S.op("pool", lambda e, W3=W3: e.memset(W3[64:96].rearrange("p a b c d -> p (a b c d)"), 0.0), writes=[B_Wd])

                        ev = [0]

                        def f_mm(j):
                            Wd, W3, B_Wd = WD[j % 2]
                            for d in range(2):
                                for q in range(4):
                                    gp = 4 * j + q
                                    for part in range(2):
                                        ps, pb = PS.next()
                                        for dl in range(8):
                                            if q < 3:
                                                lw = Wd[32 * q:32 * q + 32, d, part, dl, :]
                                                ru = u5r[32 * q:32 * q + 32, j, dl, :]
                                            else:
                                                lw = W3[64:128, d, part, dl, :]
                                                ru = u5r[64:128, j, dl, :]
                                            S.op("pe", lambda e, ps=ps, lw=lw, ru=ru, dl=dl: e.matmul(ps[:, :], lhsT=lw, rhs=ru, start=(dl == 0), stop=(dl == 7)),
                                                 reads=[B_Wd] + B_u5[j], writes=[pb])
                                        if ev[0] % 2 == 0:
                                            S.op("act", lambda e, ps=ps, d=d, part=part, gp=gp: e.activation(out=FS[:, d, part, gp, :], in_=ps[:, :], func=AF.Copy), reads=[pb], writes=[B_FS[d]])
                                        else:
                                            S.op("dve", lambda e, ps=ps, d=d, part=part, gp=gp: e.tensor_copy(out=FS[:, d, part, gp, :], in_=ps[:, :]), reads=[pb], writes=[B_FS[d]])
                                        ev[0] += 1

                        build_w(0)
                        for j in range(4):
                            if j + 1 < 4:
                                build_w(j + 1)
                            f_mm(j)
                    S.barrier()
                    if debug and "F" in dbg_out:
                        with ExitStack() as pd:
                            tmp = sbt(pd, "dbg_F", [128, 2 * 2 * 16 * 512], F32); tb_ = Buf()
                            S.op("dve", lambda e: e.tensor_copy(out=tmp[:], in_=FS[:].rearrange("p a b c d -> p (a b c d)")), writes=[tb_])
                            S.op("sp", lambda e: e.dma_start(out=dbg_out["F"], in_=tmp[:]), reads=[tb_], writes=[B_dbg], dma=True)
                        S.barrier()

                    with ExitStack() as pr:
                        NSEG, SL = 8, 64
                        A1 = sbt(pr, "A1", [128, 2, 2, 16], F32)
                        A2 = sbt(pr, "A2", [128, 2, 2, 16], F32)
                        for d in range(2):
                            cs = slice(d * 16, (d + 1) * 16)
                            for part in range(2):
                                vop(lambda e, d=d, part=part, cs=cs: e.tensor_copy(out=A1[:, d, part, :], in_=Pre[:, 8, cs]))
                            vop(lambda e, d=d, cs=cs: e.tensor_scalar(out=A2[:, d, 0, :], in0=Pim[:, 8, cs], scalar1=-1.0, scalar2=None, op0=ALU.mult))
                            vop(lambda e, d=d, cs=cs: e.tensor_copy(out=A2[:, d, 1, :], in_=Pim[:, 8, cs]))
                        S.barrier()
                        for d in range(2):
                            eng_ = "dve" if d == 0 else "pool"
                            cs = slice(d * 16, (d + 1) * 16)
                            R = sbt(pr, "R%d" % d, [128, 2, 16, NSEG], F32); Rb = Buf()
                            t1 = sbt(pr, "t1%d" % d, [128, 2, 16, NSEG], F32); t1b = Buf()
                            t2 = sbt(pr, "t2%d" % d, [128, 2, 16, NSEG], F32); t2b = Buf()
                            APr = sbt(pr, "APr%d" % d, [128, 16, SL], F32)
                            APi = sbt(pr, "APi%d" % d, [128, 16, SL], F32)
                            Bap = Buf()
                            D1 = sbt(pr, "D1%d" % d, [128, 16, SL], F32); D1b = Buf()
                            D2 = sbt(pr, "D2%d" % d, [128, 16, SL], F32); D2b = Buf()
                            Et = sbt(pr, "Et%d" % d, [128, 2, 16], F32); Etb = Buf()
                            APiN = sbt(pr, "APiN%d" % d, [128, 2, 16, SL], F32); Bapn = Buf()
                            T1 = sbt(pr, "T1%d" % d, [128, 2, 16, SL], F32); T1b = Buf()
                            T2 = sbt(pr, "T2%d" % d, [128, 2, 16, SL], F32); T2b = Buf()
                            e1 = sbt(pr, "e1%d" % d, [128, 2, 16], F32); e1b = Buf()
                            e2 = sbt(pr, "e2%d" % d, [128, 2, 16], F32); e2b = Buf()

                            cur_eng = [eng_]

                            def X(fn, reads, writes, cur_eng=cur_eng):
                                S.op(cur_eng[0], fn, reads=reads, writes=writes)

                            cur_eng[0] = "dve" if d == 0 else "pool"

                            X(lambda e: e.tensor_copy(out=APr[:, :, 0], in_=Pre[:, 8, cs]), [Bp], [Bap])
                            X(lambda e: e.tensor_copy(out=APi[:, :, 0], in_=Pim[:, 8, cs]), [Bp], [Bap])
                            n_ = 1
                            while n_ < SL:
                                sr = APr[:, :, n_ - 1:n_].to_broadcast([128, 16, n_])
                                si = APi[:, :, n_ - 1:n_].to_broadcast([128, 16, n_])
                                X(lambda e, n_=n_, sr=sr: e.tensor_tensor(out=D1[:, :, 0:n_], in0=APr[:, :, 0:n_], in1=sr, op=ALU.mult), [Bap], [D1b])
                                X(lambda e, n_=n_, si=si: e.tensor_tensor(out=D2[:, :, 0:n_], in0=APi[:, :, 0:n_], in1=si, op=ALU.mult), [Bap], [D2b])
                                X(lambda e, n_=n_: e.tensor_tensor(out=APr[:, :, n_:2 * n_], in0=D1[:, :, 0:n_], in1=D2[:, :, 0:n_], op=ALU.subtract), [D1b, D2b], [Bap])
                                X(lambda e, n_=n_, si=si: e.tensor_tensor(out=D1[:, :, 0:n_], in0=APr[:, :, 0:n_], in1=si, op=ALU.mult), [Bap], [D1b])
                                X(lambda e, n_=n_, sr=sr: e.tensor_tensor(out=D2[:, :, 0:n_], in0=APi[:, :, 0:n_], in1=sr, op=ALU.mult), [Bap], [D2b])
                                X(lambda e, n_=n_: e.tensor_tensor(out=APi[:, :, n_:2 * n_], in0=D1[:, :, 0:n_], in1=D2[:, :, 0:n_], op=ALU.add), [D1b, D2b], [Bap])
                                n_ *= 2
                            X(lambda e: e.tensor_scalar(out=APiN[:, 0], in0=APi[:], scalar1=-1.0, scalar2=None, op0=ALU.mult), [Bap], [Bapn])
                            X(lambda e: e.tensor_copy(out=APiN[:, 1], in_=APi[:]), [Bap], [Bapn])
                            cur_eng[0] = eng_
                            FSv = FS[:, d].rearrange("p a g (s i) -> p a g s i", s=NSEG)
                            A1b = A1[:, d].unsqueeze(3).to_broadcast([128, 2, 16, NSEG])
                            A2b = A2[:, d].unsqueeze(3).to_broadcast([128, 2, 16, NSEG])
                            for i in range(1, SL):
                                ii = i if d == 0 else SL - 1 - i
                                pi = ii - 1 if d == 0 else ii + 1
                                X(lambda e, pi=pi: e.tensor_tensor(out=t1[:], in0=FSv[:, :, :, :, pi], in1=A1b, op=ALU.mult), [B_FS[d]], [t1b])
                                X(lambda e, pi=pi: e.tensor_tensor(out=t2[:], in0=FSv[:, ::-1, :, :, pi], in1=A2b, op=ALU.mult), [B_FS[d]], [t2b])
                                X(lambda e: e.tensor_tensor(out=t1[:], in0=t1[:], in1=t2[:], op=ALU.add), [t1b, t2b], [t1b])
                                X(lambda e, ii=ii: e.tensor_tensor(out=FSv[:, :, :, :, ii], in0=FSv[:, :, :, :, ii], in1=t1[:], op=ALU.add), [t1b, B_FS[d]], [B_FS[d]])
                            last = SL - 1 if d == 0 else 0
                            X(lambda e, last=last: e.tensor_copy(out=R[:], in_=FSv[:, :, :, :, last]), [B_FS[d]], [Rb])
                            cur_eng[0] = "dve"
                            segs = list(range(NSEG)) if d == 0 else list(range(NSEG - 1, -1, -1))
                            X(lambda e: e.tensor_copy(out=Et[:], in_=FSv[:, :, :, segs[0], last]), [B_FS[d]], [Etb])
                            APrb = APr[:].unsqueeze(1).to_broadcast([128, 2, 16, SL])
                            for si_, sg in enumerate(segs[1:]):
                                if d == 0:
                                    fs = FS[:, d, :, :, sg * SL:(sg + 1) * SL]
                                else:
                                    lo_ = sg * SL
                                    fs = FS[:, d, :, :, lo_ + SL - 1:(lo_ - 1 if lo_ > 0 else None):-1]
                                eb = Et[:].unsqueeze(3).to_broadcast([128, 2, 16, SL])
                                esw = Et[:, ::-1, :].unsqueeze(3).to_broadcast([128, 2, 16, SL])
                                X(lambda e, eb=eb: e.tensor_tensor(out=T1[:], in0=APrb, in1=eb, op=ALU.mult), [Bap, Etb], [T1b])
                                X(lambda e, esw=esw: e.tensor_tensor(out=T2[:], in0=APiN[:], in1=esw, op=ALU.mult), [Bapn, Etb], [T2b])
                                X(lambda e: e.tensor_tensor(out=T1[:], in0=T1[:], in1=T2[:], op=ALU.add), [T1b, T2b], [T1b])
                                X(lambda e, fs=fs: e.tensor_tensor(out=fs, in0=fs, in1=T1[:], op=ALU.add), [T1b, B_FS[d]], [B_FS[d]])
                                if si_ < NSEG - 2:
                                    X(lambda e, sg=sg: e.tensor_copy(out=Et[:], in_=FSv[:, :, :, sg, last]), [B_FS[d]], [Etb])
                    S.barrier()
                    if debug and "S" in dbg_out:
                        with ExitStack() as pd:
                            tmp = sbt(pd, "dbg_S", [128, 2 * 2 * 16 * 512], F32); tb_ = Buf()
                            S.op("dve", lambda e: e.tensor_copy(out=tmp[:], in_=FS[:].rearrange("p a b c d -> p (a b c d)")), writes=[tb_])
                            S.op("sp", lambda e: e.dma_start(out=dbg_out["S"], in_=tmp[:]), reads=[tb_], writes=[B_dbg], dma=True)
                        S.barrier()

                    with ExitStack() as po:
                        XB = sbt(po, "XB", [128, 2, 32, 2, 16], F32)
                        B_XB = Buf("XB")
                        for part in range(2):
                            for gl in range(2):
                                S.op("dve", lambda e, part=part, gl=gl: e.tensor_scalar(out=XB[:, part, :, gl, :], in0=BB[:, part], scalar1=maskg[:, gl:gl + 1], scalar2=None, op0=ALU.mult),
                                     reads=[Bp, B_const], writes=[B_XB])
                        CA = sbt(po, "CA", [128, 2, 9, 8, 16], F32)
                        B_CA = Buf("CA")
                        tC = sbt(po, "tC", [128, 9, 4, 16], F32); tCb = Buf()
                        tD = sbt(po, "tD", [128, 9, 4, 16], F32); tDb = Buf()
                        XC = sbt(po, "XC", [128, 9, 2, 8, 2, 16], F32)
                        B_XC = Buf("XC")
                        CCW = [(sbt(po, "Ccw%d" % i, [128, 36, 5, 32], BF16), Buf("Ccw%d" % i)) for i in range(2)]
                        KW = [(sbt(po, "Kw%d" % i, [128, 2, 8, 128], BF16), Buf("Kw%d" % i)) for i in range(2)]
                        XBp = sbt(po, "XBp", [128, 2, 4, 128], F32)
                        B_XBp = Buf("XBp")
                        S.op("pool", lambda e: e.memset(XBp[:].rearrange("p a b c -> p (a b c)"), 0.0), writes=[B_XBp])
                        for i in range(2):
                            S.op("pool", lambda e, i=i: e.memset(CCW[i][0][:].rearrange("p a b c -> p (a b c)"), 0.0), writes=[CCW[i][1]])
                        y32 = [(sbt(po, "y32_%d" % i, [128, 512], F32), Buf()) for i in range(2)]
                        yt = [(sbt(po, "yt_%d" % i, [128, 512], F32), Buf()) for i in range(2)]
                        y1s = [(sbt(po, "y1s_%d" % i, [128, 512], BF16), Buf()) for i in range(2)]

                        def build_y(j):
                            Ccw, B_Ccw = CCW[j % 2]
                            Kw, B_Kw = KW[j % 2]
                            for d in range(2):
                                cs = slice(d * 16 + 4 * j, d * 16 + 4 * j + 4)
                                ds = slice(d * 4, d * 4 + 4)
                                sr = Pre[:, 0:9, cs].unsqueeze(3).to_broadcast([128, 9, 4, 16])
                                si = Pim[:, 0:9, cs].unsqueeze(3).to_broadcast([128, 9, 4, 16])
                                xr = CC[:, 0, cs].unsqueeze(1).to_broadcast([128, 9, 4, 16])
                                xi = CC[:, 1, cs].unsqueeze(1).to_broadcast([128, 9, 4, 16])
                                S.op("dve", lambda e, xr=xr, sr=sr: e.tensor_tensor(out=tC[:], in0=xr, in1=sr, op=ALU.mult), reads=[Bp], writes=[tCb])
                                S.op("dve", lambda e, xi=xi, si=si: e.tensor_tensor(out=tD[:], in0=xi, in1=si, op=ALU.mult), reads=[Bp], writes=[tDb])
                                S.op("dve", lambda e, ds=ds: e.tensor_tensor(out=CA[:, 0, :, ds, :], in0=tC[:], in1=tD[:], op=ALU.subtract), reads=[tCb, tDb], writes=[B_CA])
                                S.op("dve", lambda e, xr=xr, si=si: e.tensor_tensor(out=tC[:], in0=xr, in1=si, op=ALU.mult), reads=[Bp], writes=[tCb])
                                S.op("dve", lambda e, xi=xi, sr=sr: e.tensor_tensor(out=tD[:], in0=xi, in1=sr, op=ALU.mult), reads=[Bp], writes=[tDb])
                                S.op("dve", lambda e, ds=ds: e.tensor_tensor(out=CA[:, 1, :, ds, :], in0=tC[:], in1=tD[:], op=ALU.add), reads=[tCb, tDb], writes=[B_CA])
                            for gl in range(2):
                                S.op("dve", lambda e, gl=gl: e.tensor_scalar(out=XC[:, :, 0, :, gl, :], in0=CA[:, 0], scalar1=maskg[:, gl:gl + 1], scalar2=None, op0=ALU.mult),
                                     reads=[B_CA, B_const], writes=[B_XC])
                                S.op("dve", lambda e, gl=gl: e.tensor_scalar(out=XC[:, :, 1, :, gl, :], in0=CA[:, 1], scalar1=maskg[:, gl:gl + 1], scalar2=-1.0, op0=ALU.mult, op1=ALU.mult),
                                     reads=[B_CA, B_const], writes=[B_XC])
                            XCv = XC[:].rearrange("p e a (d q) g f -> p (e a d) q (g f)", d=2)
                            S.op("act", lambda e, XCv=XCv, Ccw=Ccw: e.activation(out=Ccw[:, :, 0:3, :], in_=XCv[:, :, 0:3, :], func=AF.Copy), reads=[B_XC], writes=[B_Ccw])
                            S.op("act", lambda e, XCv=XCv, Ccw=Ccw: e.activation(out=Ccw[:, :, 4, :], in_=XCv[:, :, 3, :], func=AF.Copy), reads=[B_XC], writes=[B_Ccw])
                            for d in range(2):
                                for part in range(2):
                                    for q in range(4):
                                        col = d * 16 + 4 * j + q
                                        S.op("pool", lambda e, part=part, q=q, col=col: e.tensor_copy(out=XBp[:, part, q, 32 * q:32 * q + 32], in_=XB[:, part, col].rearrange("p a b -> p (a b)")),
                                             reads=[B_XB], writes=[B_XBp])
                                for k in range(8):
                                    ps, pb = PS.next()
                                    for q in range(4):
                                        for part in range(2):
                                            S.op("pe", lambda e, ps=ps, q=q, part=part, k=k, d=d: e.matmul(
                                                ps[:, 32 * q:32 * q + 32],
                                                lhsT=XBp[:, part, q, :],
                                                rhs=XC[:, k, part, d * 4 + q].rearrange("p a b -> p (a b)"),
                                                start=(part == 0), stop=(part == 1)), reads=[B_XBp, B_XC], writes=[pb])
                                    S.op("act", lambda e, ps=ps, d=d, k=k, Kw=Kw: e.activation(out=Kw[:, d, k, :], in_=ps[:, 0:128], func=AF.Copy), reads=[pb], writes=[B_Kw])

                        def mm_y(j, tb):
                            Ccw, B_Ccw = CCW[j % 2]
                            Kw, B_Kw = KW[j % 2]
                            sl = slice(tb * 512, (tb + 1) * 512)
                            ps, pb = PS.next()
                            cs_ = slice(tb * 64, (tb + 1) * 64)
                            rd = [B_Kw] + B_u5[j] + [B_Ccw] + B_FS
                            S.op("pe", lambda e: e.matmul(ps[:, :], lhsT=Kw[:, 0, 0, :], rhs=u5r[:, j, :, cs_], start=True, stop=False), reads=rd, writes=[pb])
                            for k in range(1, 8):
                                S.op("pe", lambda e, k=k: e.matmul(ps[:, k * 64:512], lhsT=Kw[:, 0, k, :], rhs=u5r[:, j, 0:8 - k, cs_], start=False, stop=False), reads=rd, writes=[pb])
                            for k in range(0, 8):
                                S.op("pe", lambda e, k=k: e.matmul(ps[:, 0:(8 - k) * 64], lhsT=Kw[:, 1, k, :], rhs=u5r[:, j, k:8, cs_], start=False, stop=False), reads=rd, writes=[pb])
                            c0g = tb * 64
                            mm = []

                            def orow(q):
                                return ps[32 * q:32 * q + 32] if q < 3 else ps[64:128]

                            def cw(e_, part, d, q):
                                i_ = (e_ * 2 + part) * 2 + d
                                return Ccw[:, i_, q, :] if q < 3 else Ccw[:, i_, 3:5, :].rearrange("p a b -> p (a b)")

                            for q in range(4):
                                gp = 4 * j + q
                                for tau in range(8):
                                    for part in range(2):
                                        lo = 1 if tb == 0 else 0
                                        mm.append((orow(q)[:, tau * 64 + lo:tau * 64 + 64], cw(tau + 1, part, 0, q), FS[:, 0, part, gp, c0g + lo - 1:c0g + 63]))
                                        hi = 63 if tb == 7 else 64
                                        mm.append((orow(q)[:, tau * 64:tau * 64 + hi], cw(8 - tau, part, 1, q), FS[:, 1, part, gp, c0g + 1:c0g + hi + 1]))
                            for i_, (o_, w_, r_) in enumerate(mm):
                                S.op("pe", lambda e, o_=o_, w_=w_, r_=r_, last=(i_ == len(mm) - 1): e.matmul(o_, lhsT=w_, rhs=r_, start=False, stop=last), reads=rd, writes=[pb])
                            y_t, yb = y32[tb % 2]
                            t_t, tb_ = yt[tb % 2]
                            o_t, ob_ = y1s[tb % 2]
                            S.op("dve", lambda e: e.scalar_tensor_tensor(out=y_t[:, :].rearrange("p (c t) -> p c t", t=8), in0=u5r[:, j, :, cs_].rearrange("p t c -> p c t"), scalar=dsk[:, j:j + 1],
                                                                         in1=ps[:, :].rearrange("p (t c) -> p c t", t=8), op0=ALU.mult, op1=ALU.add),
                                 reads=[pb, Bp] + B_u5[j], writes=[yb])
                            S.op("pool", lambda e: e.tensor_tensor(out=t_t[:, :], in0=y_t[:, :], in1=y_t[:, :], op=ALU.mult), reads=[yb], writes=[tb_])
                            S.op("pool", lambda e: e.tensor_scalar(out=t_t[:, :], in0=t_t[:, :], scalar1=0.044715, scalar2=1.0, op0=ALU.mult, op1=ALU.add), reads=[tb_], writes=[tb_])
                            S.op("pool", lambda e: e.tensor_tensor(out=t_t[:, :], in0=t_t[:, :], in1=y_t[:, :], op=ALU.mult), reads=[yb, tb_], writes=[tb_])
                            S.op("act", lambda e: e.activation(out=t_t[:, :], in_=t_t[:, :], func=AF.Sigmoid, scale=2.0 * math.sqrt(2.0 / math.pi)), reads=[tb_], writes=[tb_])
                            S.op("dve", lambda e: e.tensor_tensor(out=o_t[:, :], in0=y_t[:, :], in1=t_t[:, :], op=ALU.mult), reads=[yb, tb_], writes=[ob_])
                            S.op("sp", lambda e: e.dma_start(out=y1_d[j * 128:(j + 1) * 128, sl], in_=o_t[:, :]), reads=[ob_], writes=[B_Y1[j][tb]], dma=True)

                        build_y(0)
                        for j in range(4):
                            for tb in range(8):
                                mm_y(j, tb)
                                if tb == 1 and j + 1 < 4:
                                    build_y(j + 1)
                    S.barrier()
                    with ExitStack() as pg:
                        wg = sbt(pg, "wg", [128, 4, 512], BF16); wgb = Buf()
                        S.op("pool", lambda e: e.dma_start(out=wg[:], in_=w_glu[l].rearrange("(k p) c -> p k c", p=128)), writes=[wgb], dma=True)
                        sg = [(sbt(pg, "sg%d" % i, [128, 512], F32), Buf()) for i in range(2)]
                        so = [(sbt(pg, "so%d" % i, [128, 4, 512], BF16), Buf()) for i in range(2)]
                        yl = [(sbt(pg, "yl%d" % i, [128, 4, 512], BF16), Buf()) for i in range(2)]
                        for tb in range(8):
                            sl = slice(tb * 512, (tb + 1) * 512)
                            so_t, sob = so[tb % 2]
                            yl_t, ylb = yl[tb % 2]
                            S.op("sp", lambda e, yl_t=yl_t, sl=sl: e.dma_start(out=yl_t[:], in_=y1_d[:, sl].rearrange("(m p) t -> p m t", p=128)), reads=[B_Y1[k][tb] for k in range(4)], writes=[ylb], dma=True)
                            for m in range(4):
                                ps, pb = PS.next()
                                for k in range(4):
                                    S.op("pe", lambda e, ps=ps, m=m, k=k, yl_t=yl_t: e.matmul(ps[:, :], lhsT=wg[:, k, m * 128:(m + 1) * 128], rhs=yl_t[:, k, :], start=(k == 0), stop=(k == 3)),
                                         reads=[wgb, ylb], writes=[pb])
                                sg_t, sgb = sg[m % 2]
                                S.op("act", lambda e, ps=ps, sg_t=sg_t: e.activation(out=sg_t[:, :], in_=ps[:, :], func=AF.Sigmoid), reads=[pb], writes=[sgb])
                                S.op("dve", lambda e, sg_t=sg_t, so_t=so_t, m=m, yl_t=yl_t: e.tensor_tensor(out=so_t[:, m, :], in0=yl_t[:, m, :], in1=sg_t[:, :], op=ALU.mult),
                                     reads=[sgb, ylb], writes=[sob])
                            S.op("sp", lambda e, so_t=so_t, sl=sl: e.dma_start(out=ys5_d[:, sl].rearrange("(m p) t -> p m t", p=128), in_=so_t[:]), reads=[sob], writes=[B_ys5[tb]], dma=True)
                    S.barrier()
            S.barrier()
            dbg_dump("ys5", ys5_d, B_ys5)
            dbg_dump("q", q_d, [B_q])
            if stop_after == "s5":
                break

            with ExitStack() as pt:
                wm = sbt(pt, "wm", [128, 48, 256], BF16); wmb = Buf()
                for i in range(6):
                    S.op("pool", lambda e, i=i: e.dma_start(out=wm[:, i * 8:(i + 1) * 8, :], in_=wmask_d[:, i * 8:(i + 1) * 8, :]), writes=[wmb], dma=True)
                qz = [[(sbt(pt, "q%d_%d" % (i, hh), [128, L], BF16), Buf()) for hh in range(2)] for i in range(2)]
                ks = [(sbt(pt, "k%d" % i, [128, L + 2 * KPAD], BF16), Buf()) for i in range(2)]
                vs = [(sbt(pt, "v%d" % i, [128, L + 2 * KPAD], BF16), Buf()) for i in range(2)]
                for i in range(2):
                    for (t_, b_) in (ks[i], vs[i]):
                        S.op("pool", lambda e, t_=t_: e.memset(t_[:, 0:KPAD], 0.0), writes=[b_])
                        S.op("pool", lambda e, t_=t_: e.memset(t_[:, KPAD + L:], 0.0), writes=[b_])
                    for hh in range(2):
                        t_, b_ = qz[i][hh]
                        zr = slice(64, 128) if hh == 0 else slice(0, 64)
                        S.op("pool", lambda e, t_=t_, zr=zr: e.memset(t_[zr, :], 0.0), writes=[b_])
                acc = sbt(pt, "acc", [65, 2, L], F32)
                B_acc = [Buf(), Buf()]
                VPs = [(sbt(pt, "Vp%d" % i, [128, 48, 2, 65], BF16), Buf()) for i in range(2)]
                Es = [(sbt(pt, "E%d" % i, [128, 512], BF16), Buf()) for i in range(5)]
                Ps_ = [(sbt(pt, "P%d" % i, [128, 512], BF16), Buf()) for i in range(5)]
                ER_, PR_ = Rot(Es), Rot(Ps_)
                rds = [(sbt(pt, "rd%d" % i, [65, 512], F32), Buf()) for i in range(2)]
                yo = [(sbt(pt, "yo%d" % i, [64, L], BF16), Buf()) for i in range(2)]
                PSS = Rot(psum[0:4])
                PSO = Rot(psum[4:6])
                mcnt = [0]
                vcnt = [0]

                def load_hp(hp):
                    k_t, kb_ = ks[hp % 2]
                    v_t, vb_ = vs[hp % 2]
                    rows = slice(hp * 128, (hp + 1) * 128)
                    for hh in range(2):
                        t_, b_ = qz[hp % 2][hh]
                        S.op("sp", lambda e, t_=t_, hh=hh: e.dma_start(out=t_[64 * hh:64 * hh + 64, :], in_=q_d[hp * 128 + 64 * hh:hp * 128 + 64 * hh + 64, :]), reads=[B_q], writes=[b_], dma=True)
                    S.op("sp", lambda e: e.dma_start(out=k_t[:, KPAD:KPAD + L], in_=k_d[rows, :]), reads=[B_k], writes=[kb_], dma=True)
                    S.op("sp", lambda e: e.dma_start(out=v_t[:, KPAD:KPAD + L], in_=v_d[rows, :]), reads=[B_v], writes=[vb_], dma=True)

                def normalise(hp):
                    for hh in range(2):
                        yo_t, yob = yo[hh]
                        for tb in range(8):
                            sl = slice(tb * 512, (tb + 1) * 512)
                            ps, pb = PSS.next()
                            S.op("pe", lambda e: e.matmul(ps[0:64, :], lhsT=ones_f[64:65, 0:64], rhs=acc[64:65, hh, sl], start=True, stop=True), reads=[B_acc[hh], B_const], writes=[pb])
                            rd_t, rdb = rds[tb % 2]
                            S.op("act", lambda e: e.activation(out=rd_t[0:64, :], in_=ps[0:64, :], func=AF.Ln), reads=[pb], writes=[rdb])
                            S.op("act", lambda e: e.activation(out=rd_t[0:64, :], in_=rd_t[0:64, :], func=AF.Exp, scale=-1.0), reads=[rdb], writes=[rdb])
                            S.op("pool", lambda e: e.tensor_tensor(out=yo_t[:, sl], in0=acc[0:64, hh, sl], in1=rd_t[0:64, :], op=ALU.mult), reads=[rdb, B_acc[hh]], writes=[yob])
                        r0 = hp * 128 + hh * 64
                        S.op("sp", lambda e: e.dma_start(out=yattn_d[r0:r0 + 64, :], in_=yo_t[:, :]), reads=[yob], writes=[B_yattn], dma=True)

                jobs = []
                jn = 0
                for hp in range(8):
                    k_t, kb_ = ks[hp % 2]
                    v_t, vb_ = vs[hp % 2]
                    for pi, D in enumerate(PATTERNS):
                        M = L // D
                        nqb = M // 128
                        nkt = nqb + 1
                        ntl = D * nkt
                        Vp, B_Vp = VPs[jn % 2]
                        jn += 1
                        job = {"vp_groups": [], "units": [], "pre": None, "post": None}
                        if pi == 0:
                            job["pre"] = (lambda hp=hp: load_hp(hp + 1)) if hp + 1 < 8 else None
                        if pi == len(PATTERNS) - 1:
                            job["post"] = (lambda hp=hp: normalise(hp))

                        def vp_init(Vp=Vp, B_Vp=B_Vp, nqb=nqb, nkt=nkt, ntl=ntl):
                            S.op("pool", lambda e: e.memset(Vp[:, 0:ntl, :, 64:65], 1.0), writes=[B_Vp])
                            S.op("pool", lambda e: e.memset(Vp[0:64, 0:ntl:nkt, :, 64:65], 0.0), writes=[B_Vp])
                            S.op("pool", lambda e: e.memset(Vp[64:128, nqb:ntl:nkt, :, 64:65], 0.0), writes=[B_Vp])
                        job["vp_groups"].append(vp_init)
                        for t0_ in range(0, ntl, 8):
                            def vp_group(t0_=t0_, Vp=Vp, B_Vp=B_Vp, D=D, nkt=nkt, ntl=ntl, v_t=v_t, vb_=vb_):
                                nj = min(8, ntl - t0_)
                                pbt, pbb = PSB.next()
                                for jo in range(nj):
                                    t_ = t0_ + jo
                                    r, jj = t_ // nkt, t_ % nkt
                                    c0 = KPAD + r + D * (128 * jj - 64)
                                    S.op("pe", lambda e, c0=c0, jo=jo: e.transpose(out=pbt[:, jo * 128:(jo + 1) * 128], in_=v_t[:, c0:c0 + 127 * D + 1:D], identity=ident_b[:, :]),
                                         reads=[vb_, B_const], writes=[pbb])
                                src = pbt[:, 0:nj * 128].rearrange("p (j h e) -> p j h e", h=2, e=64)
                                vi = vcnt[0]
                                vcnt[0] += 1
                                if vi % 2 == 0:
                                    S.op("act", lambda e: e.activation(out=Vp[:, t0_:t0_ + nj, :, 0:64], in_=src, func=AF.Copy), reads=[pbb], writes=[B_Vp])
                                else:
                                    S.op("dve", lambda e: e.tensor_copy(out=Vp[:, t0_:t0_ + nj, :, 0:64], in_=src), reads=[pbb], writes=[B_Vp])
                            job["vp_groups"].append(vp_group)

                        gsz = min(4, nqb)
                        for r in range(D):
                            for hh in range(2):
                                q_t, qb_ = qz[hp % 2][hh]
                                widx = pi * 16 + hp * 2 + hh
                                ostate = {"po": None, "pob": None}
                                for j0 in range(0, nkt, 2):
                                    tiles = [j for j in (j0, j0 + 1) if j < nkt]
                                    ctx = {}

                                    def stage1(tiles=tiles, j0=j0, r=r, widx=widx, ctx=ctx, D=D, nqb=nqb, q_t=q_t, qb_=qb_, k_t=k_t, kb_=kb_):
                                        ps, pb = PSS.next()
                                        for j in tiles:
                                            qlo, qhi = max(j - 1, 0), min(j, nqb - 1)
                                            nq = (qhi - qlo + 1) * 128
                                            qc0 = r + D * 128 * qlo
                                            kc0 = KPAD + r + D * (128 * j - 64)
                                            so = 256 * (j - j0) + (128 if j == 0 else 0)
                                            S.op("pe", lambda e, kc0=kc0, qc0=qc0, nq=nq, so=so: e.matmul(
                                                ps[:, so:so + nq], lhsT=k_t[:, kc0:kc0 + 127 * D + 1:D], rhs=q_t[:, qc0:qc0 + (nq - 1) * D + 1:D], start=True, stop=True),
                                                reads=[kb_, qb_], writes=[pb])
                                        c_lo = 128 if j0 == 0 else 0
                                        c_hi = 256 * (len(tiles) - 1) + (128 if tiles[-1] == nqb else 256)
                                        e_t, eb = ER_.next()
                                        p_t, pbf = PR_.next()
                                        ctx["p"] = (p_t, pbf)
                                        S.op("act", lambda e: e.activation(out=e_t[:, c_lo:c_hi], in_=ps[:, c_lo:c_hi], func=AF.Exp), reads=[pb], writes=[eb])
                                        me = "pool" if mcnt[0] % 3 == 2 else "dve"
                                        mcnt[0] += 1
                                        if c_lo == 0 and c_hi == 512:
                                            S.op(me, lambda e: e.tensor_tensor(out=p_t[:, :].rearrange("p (a b) -> p a b", a=2), in0=e_t[:, :].rearrange("p (a b) -> p a b", a=2),
                                                                               in1=wm[:, widx:widx + 1, :].to_broadcast([128, 2, 256]), op=ALU.mult),
                                                 reads=[eb, wmb], writes=[pbf])
                                        else:
                                            for j in tiles:
                                                so = 256 * (j - j0)
                                                lo_ = 128 if j == 0 else 0
                                                hi_ = 128 if j == nqb else 256
                                                S.op(me, lambda e, so=so, lo_=lo_, hi_=hi_: e.tensor_tensor(out=p_t[:, so + lo_:so + hi_], in0=e_t[:, so + lo_:so + hi_],
                                                                                                           in1=wm[:, widx, lo_:hi_], op=ALU.mult),
                                                     reads=[eb, wmb], writes=[pbf])

                                    def stage2(tiles=tiles, j0=j0, r=r, hh=hh, ctx=ctx, D=D, nqb=nqb, nkt=nkt, gsz=gsz, Vp=Vp, B_Vp=B_Vp, ostate=ostate, pi=pi):
                                        p_t, pbf = ctx["p"]
                                        for j in tiles:
                                            so = 256 * (j - j0)
                                            parts = []
                                            if j >= 1:
                                                parts.append((j - 1, so))
                                            if j <= nqb - 1:
                                                parts.append((j, so + 128))
                                            if len(parts) == 2 and (j - 1) // gsz == j // gsz:
                                                groups = [((j - 1), so, 256)]
                                            else:
                                                groups = [(qb, c_, 128) for (qb, c_) in parts]
                                            for (qb, c_, n_) in groups:
                                                if qb % gsz == 0 and c_ == so + 128:
                                                    ostate["po"], ostate["pob"] = PSO.next()
                                                    first = True
                                                else:
                                                    first = False
                                                po_, pob = ostate["po"], ostate["pob"]
                                                last = (n_ == 128 and c_ == so and qb % gsz == gsz - 1)
                                                oc = (qb % gsz) * 128
                                                S.op("pe", lambda e, oc=oc, n_=n_, j=j, c_=c_, first=first, last=last, po_=po_: e.matmul(
                                                    po_[0:65, oc:oc + n_], lhsT=Vp[:, r * nkt + j, hh, :], rhs=p_t[:, c_:c_ + n_], start=first, stop=last, skip_group_check=True),
                                                    reads=[B_Vp, pbf], writes=[pob])
                                                if last:
                                                    g0 = (qb // gsz) * gsz
                                                    t0 = r + D * 128 * g0
                                                    nn = gsz * 128
                                                    dst = acc[0:65, hh, t0:t0 + (nn - 1) * D + 1:D]
                                                    if pi == 0:
                                                        S.op("act", lambda e, dst=dst, po_=po_, nn=nn: e.activation(out=dst, in_=po_[0:65, 0:nn], func=AF.Copy), reads=[pob], writes=[B_acc[hh]])
                                                    else:
                                                        S.op("dve", lambda e, dst=dst, po_=po_, nn=nn: e.tensor_tensor(out=dst, in0=po_[0:65, 0:nn], in1=dst, op=ALU.add), reads=[pob, B_acc[hh]], writes=[B_acc[hh]])

                                    job["units"].append((stage1, stage2))
                        jobs.append(job)

                LAG = 4
                load_hp(0)
                for g in jobs[0]["vp_groups"]:
                    g()
                flat = []
                for ji, job in enumerate(jobs):
                    for ui, (s1, s2) in enumerate(job["units"]):
                        flat.append((ji, ui, s1, s2))
                pending = {}
                for fi in range(len(flat) + LAG):
                    if fi < len(flat):
                        ji, ui, s1, s2 = flat[fi]
                        job = jobs[ji]
                        if ui == 0 and job["pre"] is not None:
                            job["pre"]()
                        if ui == LAG and ji + 1 < len(jobs):
                            pending[ji] = list(jobs[ji + 1]["vp_groups"])
                        s1()
                        if pending.get(ji):
                            pending[ji].pop(0)()
                        if ui == len(job["units"]) - 1:
                            while pending.get(ji):
                                pending[ji].pop(0)()
                    if fi >= LAG:
                        ji, ui, s1, s2 = flat[fi - LAG]
                        s2()
                        if ui == len(jobs[ji]["units"]) - 1 and jobs[ji]["post"] is not None:
                            jobs[ji]["post"]()
            S.barrier()
            dbg_dump("yattn", yattn_d, [B_yattn])
            if stop_after == "attn":
                break

            with ExitStack() as pc:
                wgt = sbt(pc, "c_wg", [128, 8, 3072], BF16)
                wb5 = sbt(pc, "c_wb5", [128, 4, DM], BF16)
                wbp = sbt(pc, "c_wbp", [128, 4, DM], BF16)
                wba = sbt(pc, "c_wba", [128, 8, DM], BF16)
                wo = sbt(pc, "c_wo", [128, 8, DM], BF16)
                Bwg = [[Buf() for _ in range(2)] for _ in range(3)]
                Bwb = [[Buf() for _ in range(2)] for _ in range(3)]
                Bwo = Buf("c1wo")
                wbrs = (wb5, wbp, wba)
                wsrc = (w_bs5, w_bpool, w_battn)
                def issue_c1_weights():
                    for hf in range(2):
                        for b in range(3):
                            c0 = 4096 + b * 1024 + hf * 512
                            S.op("pool", lambda e, b=b, hf=hf, c0=c0: e.dma_start(out=wgt[:, :, b * 1024 + hf * 512:b * 1024 + (hf + 1) * 512],
                                                                                in_=w_in[l].rearrange("(k p) c -> p k c", p=128)[:, :, c0:c0 + 512]), writes=[Bwg[b][hf]], dma=True)
                            S.op("pool", lambda e, b=b, hf=hf: e.dma_start(out=wbrs[b][:, :, hf * 512:(hf + 1) * 512],
                                                                          in_=wsrc[b][l].rearrange("(k p) c -> p k c", p=128)[:, :, hf * 512:(hf + 1) * 512]), writes=[Bwb[b][hf]], dma=True)
                    for k in range(8):
                        S.op("pool", lambda e, k=k: e.dma_start(out=wo[:, k, :], in_=w_out[l, k * 128:(k + 1) * 128, :]), writes=[Bwo], dma=True)
                with ExitStack() as pp:
                    up = sbt(pp, "upool", [128, 4, L + 16], BF16)
                    B_up = [Buf() for _ in range(4)]
                    for g in range(4):
                        S.op("dve", lambda e, g=g: e.memset(up[:, g, 0:8], 0.0), writes=[B_up[g]])
                        S.op("dve", lambda e, g=g: e.memset(up[:, g, L + 8:L + 16], 0.0), writes=[B_up[g]])
                        S.op("sp", lambda e, g=g: e.dma_start(out=up[:, g, 8:8 + L], in_=upool_d[g * 128:(g + 1) * 128, :]), reads=[B_upd], writes=[B_up[g]], dma=True)
                    pw = sbt(pp, "p_w", [128, 4, 128], BF16); pwb = Buf()
                    psc = sbt(pp, "p_sc", [128, 4], F32)
                    S.op("pool", lambda e: e.dma_start(out=pw[:], in_=pool_w[l].rearrange("g c d -> c g d")), writes=[pwb], dma=True)
                    S.op("sp", lambda e: e.dma_start(out=psc[:], in_=pool_scale[:, l, :]), writes=[pwb], dma=True)
                    issue_c1_weights()
                    ta = sbt(pp, "p_ta", [128, L + 16], F32); tab = Buf()
                    tb2 = sbt(pp, "p_tb", [128, L + 16], F32); tbb = Buf()
                    pl = sbt(pp, "p_pl", [128, L], BF16); plb = Buf()
                    stp = [(sbt(pp, "p_st%d" % i, [128, 512], BF16), Buf()) for i in range(2)]
                    O = 8
                    for g in range(4):
                        win = POOL_WINS[g]
                        u = up[:, g, :]
                        ub = B_up[g]
                        S.op("dve", lambda e, u=u: e.tensor_tensor(out=ta[:, O - 7:O + L + 8], in0=u[:, O - 7:O + L + 8], in1=u[:, O - 8:O + L + 7], op=ALU.add), reads=[ub], writes=[tab])
                        cur, curb, oth, othb = ta, tab, tb2, tbb
                        lo, hi = -7, L + 8
                        sh = 1
                        w_ = 2
                        while w_ < win:
                            lo2, hi2 = lo + sh, hi - sh
                            S.op("dve", lambda e, cur=cur, oth=oth, lo2=lo2, hi2=hi2, sh=sh: e.tensor_tensor(out=oth[:, O + lo2:O + hi2], in0=cur[:, O + lo2 + sh:O + hi2 + sh],
                                                                                                          in1=cur[:, O + lo2 - sh:O + hi2 - sh], op=ALU.add), reads=[curb], writes=[othb])
                            cur, curb, oth, othb = oth, othb, cur, curb
                            lo, hi = lo2, hi2
                            sh *= 2
                            w_ *= 2
                        S.op("dve", lambda e, cur=cur, u=u, win=win: e.scalar_tensor_tensor(out=pl[:, :], in0=cur[:, O:O + L], scalar=1.0 / win, in1=u[:, O:O + L], op0=ALU.mult, op1=ALU.subtract),
                             reads=[curb, ub], writes=[plb])
                        left = win // 2
                        right = win - 1 - left
                        for t in range(L):
                            hi_ = min(t + right + 1, L)
                            lo_ = max(t - left, 0)
                            if hi_ - lo_ != win:
                                S.op("dve", lambda e, cur=cur, u=u, t=t, c_=float(hi_ - lo_): e.scalar_tensor_tensor(out=pl[:, t:t + 1], in0=cur[:, O + t:O + t + 1], scalar=1.0 / c_,
                                                                                                               in1=u[:, O + t:O + t + 1], op0=ALU.mult, op1=ALU.subtract),
                                     reads=[curb, ub], writes=[plb])
                        for tt in range(8):
                            sl = slice(tt * 512, (tt + 1) * 512)
                            ps, pb = PS.next()
                            S.op("pe", lambda e, ps=ps, g=g, sl=sl: e.matmul(ps[:, :], lhsT=pw[:, g, :], rhs=pl[:, sl], start=True, stop=True), reads=[pwb, plb], writes=[pb])
                            st_t, stb = stp[tt % 2]
                            S.op("act", lambda e, ps=ps, st_t=st_t, g=g: e.activation(out=st_t[:, :], in_=ps[:, :], func=AF.Copy, scale=psc[:, g:g + 1]), reads=[pb, pwb], writes=[stb])
                            S.op("sp", lambda e, st_t=st_t, g=g, sl=sl: e.dma_start(out=ypool_d[g * 128:(g + 1) * 128, sl], in_=st_t[:, :]), reads=[stb], writes=[B_ypool[tt]], dma=True)
                S.barrier()
                INB = [dict(xn=(sbt(pc, "c_xn%d" % i, [128, 8, 512], BF16), Buf()), y5=(sbt(pc, "c_y5%d" % i, [128, 4, 512], BF16), Buf()),
                            yp=(sbt(pc, "c_yp%d" % i, [128, 4, 512], BF16), Buf()), ya=(sbt(pc, "c_ya%d" % i, [128, 8, 512], BF16), Buf())) for i in range(2)]

                def c1_loads(tt):
                    sl = slice(tt * 512, (tt + 1) * 512)
                    ib = INB[tt % 2]
                    S.op("sp", lambda e: e.dma_start(out=ib["xn"][0][:], in_=r_kpt(xn_d)[:, :, sl]), reads=[B_xn[2 * tt], B_xn[2 * tt + 1]], writes=[ib["xn"][1]], dma=True)
                    S.op("sp", lambda e: e.dma_start(out=ib["y5"][0][:], in_=r_kpt(ys5_d)[:, :, sl]), reads=[B_ys5[tt]], writes=[ib["y5"][1]], dma=True)
                    S.op("sp", lambda e: e.dma_start(out=ib["yp"][0][:], in_=r_kpt(ypool_d)[:, :, sl]), reads=[B_ypool[tt]], writes=[ib["yp"][1]], dma=True)
                    S.op("sp", lambda e: e.dma_start(out=ib["ya"][0][:], in_=r_kpt(yattn_d)[:, :, sl]), reads=[B_yattn], writes=[ib["ya"][1]], dma=True)

                c1_loads(0)
                h_t = sbt(pc, "c_h", [128, 8, 512], F32); hb = Buf()
                mg_t = sbt(pc, "c_mg", [128, 8, 512], BF16); mgb_all = Buf(); mgb = [mgb_all] * 8
                gts = [(sbt(pc, "c_g%d" % i, [128, 512], F32), Buf()) for i in range(2)]
                GR = Rot(gts)
                acs = [(sbt(pc, "c_ac%d" % i, [128, 512], F32), Buf()) for i in range(2)]
                hn_t = sbt(pc, "c_hn", [128, 8, 512], BF16); hnb = Buf()
                rs_t = sbt(pc, "c_rs", [128, 512], F32); rsb = Buf()
                for tt in range(8):
                    sl = slice(tt * 512, (tt + 1) * 512)
                    if tt + 1 < 8:
                        c1_loads(tt + 1)
                    ib = INB[tt % 2]
                    xn_t, xnb = ib["xn"]
                    y5_t, y5b = ib["y5"]
                    yp_t, ypb = ib["yp"]
                    ya_t, yab = ib["ya"]
                    S.op("sp", lambda e, sl=sl: e.dma_start(out=h_t[:], in_=r_kpt(hT)[:, :, sl]), reads=[B_hT[2 * tt], B_hT[2 * tt + 1]], writes=[hb], dma=True)
                    branches = ((wb5, y5_t, y5b, 4), (wbp, yp_t, ypb, 4), (wba, ya_t, yab, 8))
                    for m in range(8):
                        ms = slice(m * 128, (m + 1) * 128)
                        ac_t, acb = acs[m % 2]
                        for b in range(3):
                            wbr, y_t, yb_, nk = branches[b]
                            ps, pb = PS.next()
                            for k in range(8):
                                S.op("pe", lambda e, ps=ps, b=b, m=m, k=k: e.matmul(ps[:, :], lhsT=wgt[:, k, b * 1024 + m * 128:b * 1024 + (m + 1) * 128], rhs=xn_t[:, k, :], start=(k == 0), stop=(k == 7)),
                                     reads=[Bwg[b][m // 4], xnb], writes=[pb])
                            g_t, gb = GR.next()
                            S.op("act", lambda e, ps=ps, g_t=g_t: e.activation(out=g_t[:, :], in_=ps[:, :], func=AF.Sigmoid), reads=[pb], writes=[gb])
                            ps2, pb2 = PS.next()
                            for k in range(nk):
                                S.op("pe", lambda e, ps2=ps2, wbr=wbr, y_t=y_t, ms=ms, k=k, nk=nk: e.matmul(ps2[:, :], lhsT=wbr[:, k, ms], rhs=y_t[:, k, :], start=(k == 0), stop=(k == nk - 1)),
                                     reads=[Bwb[b][m // 4], yb_], writes=[pb2])
                            if b == 0:
                                S.op("dve", lambda e, ps2=ps2, g_t=g_t, ac_t=ac_t: e.tensor_tensor(out=ac_t[:, :], in0=ps2[:, :], in1=g_t[:, :], op=ALU.mult), reads=[pb2, gb], writes=[acb])
                            else:
                                S.op("dve", lambda e, ps2=ps2, g_t=g_t: e.tensor_tensor(out=g_t[:, :], in0=ps2[:, :], in1=g_t[:, :], op=ALU.mult), reads=[pb2, gb], writes=[gb])
                                if b == 1:
                                    S.op("pool", lambda e, g_t=g_t, ac_t=ac_t: e.tensor_tensor(out=ac_t[:, :], in0=ac_t[:, :], in1=g_t[:, :], op=ALU.add), reads=[gb, acb], writes=[acb])
                                else:
                                    S.op("pool", lambda e, g_t=g_t, ac_t=ac_t, m=m: e.tensor_tensor(out=mg_t[:, m, :], in0=ac_t[:, :], in1=g_t[:, :], op=ALU.add), reads=[gb, acb], writes=[mgb[m]])
                    for m in range(8):
                        ms = slice(m * 128, (m + 1) * 128)
                        ps, pb = PS.next()
                        for k in range(8):
                            S.op("pe", lambda e, ps=ps, ms=ms, k=k: e.matmul(ps[:, :], lhsT=wo[:, k, ms], rhs=mg_t[:, k, :], start=(k == 0), stop=(k == 7)), reads=[Bwo, mgb[k]], writes=[pb])
                        S.op("dve", lambda e, ps=ps, m=m: e.tensor_tensor(out=h_t[:, m, :], in0=ps[:, :], in1=h_t[:, m, :], op=ALU.add), reads=[pb, hb], writes=[hb])
                    S.op("sp", lambda e, sl=sl: e.dma_start(out=r_kpt(hT)[:, :, sl], in_=h_t[:]), reads=[hb], writes=[B_hT[2 * tt], B_hT[2 * tt + 1]], dma=True)
                    emit_norm(h_t, hb, gmlp[:, l, :], hn_t, hnb, mg_t, mgb_all, rs_t, rsb, 512)
                    S.op("sp", lambda e, sl=sl: e.dma_start(out=r_kpt(xn_d)[:, :, sl], in_=hn_t[:]), reads=[hnb], writes=[B_xn[2 * tt], B_xn[2 * tt + 1]], dma=True)
            S.barrier()
            dbg_dump("h1", hT, B_hT)
            dbg_dump("hn", xn_d, B_xn)
            if stop_after == "c1":
                break

            with ExitStack() as pf:
                wu = sbt(pf, "f_wu", [128, 8, 4096], BF16)
                wd = sbt(pf, "f_wd", [128, 32, DM], BF16)
                Bwu = [Buf() for _ in range(8)]
                Bwd = [Buf() for _ in range(32)]
                for cb in range(8):
                    S.op("pool", lambda e, cb=cb: e.dma_start(out=wu[:, :, cb * 512:(cb + 1) * 512], in_=w_up[l].rearrange("(k p) c -> p k c", p=128)[:, :, cb * 512:(cb + 1) * 512]), writes=[Bwu[cb]], dma=True)
                for f in range(32):
                    S.op("pool", lambda e, f=f: e.dma_start(out=wd[:, f, :], in_=w_down[l, f * 128:(f + 1) * 128, :]), writes=[Bwd[f]], dma=True)
                NT = 512
                hn_t = sbt(pf, "f_hn", [128, 8, NT], BF16); hnb = Buf()
                h_t = sbt(pf, "f_h", [128, 8, NT], F32); hb = Buf()
                act_t = sbt(pf, "f_act", [128, 32, NT], BF16); actb = [Buf() for _ in range(32)]
                rl = [(sbt(pf, "f_rl%d" % i, [128, NT], F32), Buf()) for i in range(2)]
                RL = Rot(rl)
                sq_t = sbt(pf, "f_sq", [128, 8, NT], BF16); sqb = Buf()
                rs_t = sbt(pf, "f_rs", [128, NT], F32); rsb = Buf()
                xo_t = sbt(pf, "f_xo", [128, 8, NT], BF16); xob = Buf()
                S.op("sp", lambda e: e.dma_start(out=hn_t[:], in_=r_kpt(xn_d)[:, :, 0:NT]), reads=[B_xn[0], B_xn[1]], writes=[hnb], dma=True)
                for tt in range(8):
                    sl = slice(tt * NT, (tt + 1) * NT)
                    S.op("sp", lambda e, sl=sl: e.dma_start(out=h_t[:], in_=r_kpt(hT)[:, :, sl]), reads=[B_hT[2 * tt], B_hT[2 * tt + 1]], writes=[hb], dma=True)
                    for f in range(32):
                        ps, pb = PS.next()
                        for k in range(8):
                            S.op("pe", lambda e, ps=ps, f=f, k=k: e.matmul(ps[:, :NT], lhsT=wu[:, k, f * 128:(f + 1) * 128], rhs=hn_t[:, k, :], start=(k == 0), stop=(k == 7)), reads=[Bwu[f // 4], hnb], writes=[pb])
                        r_t, rb = RL.next()
                        S.op("act", lambda e, ps=ps, r_t=r_t: e.activation(out=r_t[:, :], in_=ps[:, :NT], func=AF.Relu), reads=[pb], writes=[rb])
                        me = "dve" if f % 2 == 0 else "pool"
                        S.op(me, lambda e, r_t=r_t, f=f: e.tensor_tensor(out=act_t[:, f, :], in0=r_t[:, :], in1=r_t[:, :], op=ALU.mult), reads=[rb], writes=[actb[f]])
                    if tt + 1 < 8:
                        S.op("sp", lambda e, tt=tt: e.dma_start(out=hn_t[:], in_=r_kpt(xn_d)[:, :, (tt + 1) * NT:(tt + 2) * NT]), reads=[B_xn[2 * tt + 2], B_xn[2 * tt + 3]], writes=[hnb], dma=True)
                    for m in range(8):
                        ms = slice(m * 128, (m + 1) * 128)
                        ps, pb = PS.next()
                        for f in range(32):
                            S.op("pe", lambda e, ps=ps, ms=ms, f=f: e.matmul(ps[:, :NT], lhsT=wd[:, f, ms], rhs=act_t[:, f, :], start=(f == 0), stop=(f == 31)), reads=[Bwd[f], actb[f]], writes=[pb])
                        S.op("dve", lambda e, ps=ps, m=m: e.tensor_tensor(out=h_t[:, m, :], in0=ps[:, :NT], in1=h_t[:, m, :], op=ALU.add), reads=[pb, hb], writes=[hb])
                    if l < DEPTH - 1:
                        S.op("sp", lambda e, sl=sl: e.dma_start(out=r_kpt(hT)[:, :, sl], in_=h_t[:]), reads=[hb], writes=[B_hT[2 * tt], B_hT[2 * tt + 1]], dma=True)
                        emit_norm(h_t, hb, gmix[:, l + 1, :], xo_t, xob, sq_t, sqb, rs_t, rsb, NT)
                        S.op("sp", lambda e, sl=sl: e.dma_start(out=r_kpt(xn_d)[:, :, sl], in_=xo_t[:]), reads=[xob], writes=[B_xn[2 * tt], B_xn[2 * tt + 1]], dma=True)
                    else:
                        emit_norm(h_t, hb, gfin[:, :], h_t, hb, sq_t, sqb, rs_t, rsb, NT)
                        S.op("sp", lambda e, sl=sl: e.dma_start(out=r_kpt(outT)[:, :, sl], in_=h_t[:]), reads=[hb], writes=[B_out], dma=True)
            S.barrier()

        counts = S.emit(es)
    return nc, counts, len(S.ops)


def _state_layout(a, has_p):
    l_, d_ = a.shape[0], a.shape[1]
    if has_p:
        a = a.reshape(l_, d_, 16, 2, 64, 16).transpose(3, 4, 0, 1, 2, 5).reshape(128, l_, 32, 16)
    else:
        a = a.reshape(l_, d_, 16, 2, 64).transpose(3, 4, 0, 1, 2).reshape(128, l_, 32)
    return np.ascontiguousarray(a)


def _attn_masks():
    slopes = np.exp2(-8.0 * np.arange(1, 17, dtype=np.float64) / 16)
    a = np.arange(128)[:, None]
    b = np.arange(128)[None, :]
    relA = np.abs(a - 64 - b)
    relB = np.abs(a + 64 - b)
    out = np.zeros((128, 48, 256), np.float32)
    for pi, D in enumerate(PATTERNS):
        for h in range(16):
            wa = np.where(relA <= 64, np.exp(-slopes[h] * D * relA), 0.0)
            wb = np.where(relB <= 64, np.exp(-slopes[h] * D * relB), 0.0)
            out[:, pi * 16 + h, 0:128] = wb
            out[:, pi * 16 + h, 128:256] = wa
    return out


def make_shared_inputs(inp):
    f = lambda k: np.ascontiguousarray(np.asarray(inp[k], dtype=np.float32))
    sh = {}
    sh["w_in"] = f("w_in")
    sh["g_mix"] = np.ascontiguousarray(f("norm_mix").reshape(DEPTH, 8, 128).transpose(2, 0, 1))
    sh["g_mlp"] = np.ascontiguousarray(f("norm_mlp").reshape(DEPTH, 8, 128).transpose(2, 0, 1))
    sh["g_fin"] = np.ascontiguousarray(f("norm_final").reshape(8, 128).transpose(1, 0))
    sh["lam_re"] = _state_layout(f("s5_lam_re"), False)
    sh["lam_im"] = _state_layout(f("s5_lam_im"), False)
    ld = f("s5_log_dt")
    sh["log_dt"] = _state_layout(np.broadcast_to(ld[..., None], ld.shape + (64,)), False)
    sh["b_re"] = _state_layout(f("s5_b_re"), True)
    sh["b_im"] = _state_layout(f("s5_b_im"), True)
    sh["c_re"] = _state_layout(f("s5_c_re").transpose(0, 1, 2, 4, 3), True)
    sh["c_im"] = _state_layout(f("s5_c_im").transpose(0, 1, 2, 4, 3), True)
    sh["s5_d"] = np.ascontiguousarray(f("s5_d").reshape(DEPTH, 4, 128).transpose(2, 0, 1))
    sh["w_glu"] = f("s5_w_glu")
    sh["pool_w"] = f("pool_w")
    sh["pool_scale"] = np.ascontiguousarray(f("pool_scale").reshape(DEPTH, 4, 128).transpose(2, 0, 1))
    sh["w_bs5"] = f("w_branch_s5")
    sh["w_bpool"] = f("w_branch_pool")
    sh["w_battn"] = f("w_branch_attn")
    sh["w_out"] = f("w_out")
    sh["w_up"] = f("w_up")
    sh["w_down"] = f("w_down")
    sh["ident"] = np.eye(128, dtype=np.float32)
    mg = np.zeros((128, 2), np.float32)
    mg[:64, 0] = 1.0
    mg[64:, 1] = 1.0
    sh["maskg"] = mg
    sh["wmask"] = _attn_masks()
    return sh


_CACHE = {}


def kernel(**inputs):
    x = np.asarray(inputs["x"], dtype=np.float32)
    if "nc" not in _CACHE:
        _CACHE["nc"] = build_program()[0]
    nc = _CACHE["nc"]
    sh = make_shared_inputs(inputs)
    in_maps = []
    for b in range(NCORES):
        m = dict(sh)
        m["xT"] = np.ascontiguousarray(x[b].T)
        in_maps.append(m)
    res = run_bass_kernel_spmd(nc, in_maps, core_ids=list(range(NCORES)))
    out = np.stack([np.ascontiguousarray(np.asarray(r["outT"]).T) for r in res.results]).astype(np.float32)
    return out
```
